# Optimizing a Trainium2 kernel written in Bass

```python
import math
import jax
import jax.numpy as jnp
from jax import lax
import numpy as np

D_MODEL = 1024
BATCH = 4
SEQ = 4096
DEPTH = 4

GRID_W = 64
CTX_LEN = 256
N_BRANCH = 4
BRANCH_W = 512
FNET_GROUPS = 4
FNET_GW = BRANCH_W // FNET_GROUPS
DIFF_HEADS = 4
DIFF_HD = 64
DIFF_VD = 2 * DIFF_HD
POOL_WINDOWS = (2, 4, 8, 16)
POOL_GW = BRANCH_W // len(POOL_WINDOWS)
NA_HEADS = 8
NA_HD = BRANCH_W // NA_HEADS
NA_KH = 8
NA_KW = 16
Q_BLOCK = 128
ROPE_BASE = 10000.0
LN_EPS = 1e-6
SUBLN_EPS = 1e-5
SPLIT_SIZES = (BRANCH_W,) * 12 + (N_BRANCH * D_MODEL,)
PROJ_W = 12 * BRANCH_W + N_BRANCH * D_MODEL

kernel_name = 'hybrid_gated_parallel_mixer_dit'


def layer_norm(x, g=None, b=None):
    x32 = x.astype(jnp.float32)
    mu = jnp.mean(x32, axis=-1, keepdims=True)
    var = jnp.mean(jnp.square(x32 - mu), axis=-1, keepdims=True)
    y = (x32 - mu) * lax.rsqrt(var + LN_EPS)
    if g is not None:
        y = y * g.astype(jnp.float32) + b.astype(jnp.float32)
    return y.astype(x.dtype)


def split_proj(z):
    idx, acc = [], 0
    for s in SPLIT_SIZES[:-1]:
        acc += s
        idx.append(acc)
    return jnp.split(z, idx, axis=-1)


def to_heads(t, tail):
    return t.reshape(t.shape[:2] + tail)


def axial_rope_angles(n):
    t = jnp.arange(n)
    rows = (t // GRID_W).astype(jnp.float32)
    cols = (t % GRID_W).astype(jnp.float32)
    nf = DIFF_HD // 4
    inv = ROPE_BASE ** (-jnp.arange(nf, dtype=jnp.float32) / nf)
    return jnp.stack([rows[:, None] * inv, cols[:, None] * inv], axis=1)


def apply_rope(x, ang):
    nf = ang.shape[-1]
    xr = x.reshape(x.shape[:-1] + (2, 2, nf))
    shape = (1, ang.shape[0]) + (1,) * (x.ndim - 3) + (2, nf)
    cos = jnp.cos(ang).reshape(shape).astype(x.dtype)
    sin = jnp.sin(ang).reshape(shape).astype(x.dtype)
    x0 = xr[..., 0, :]
    x1 = xr[..., 1, :]
    out = jnp.stack([x0 * cos - x1 * sin, x1 * cos + x0 * sin], axis=-2)
    return out.reshape(x.shape)


def fourier_mix(u, w_grp):
    b, n, _ = u.shape
    ug = u.astype(jnp.float32).reshape(b, n, FNET_GROUPS, FNET_GW)
    f = jnp.fft.fft2(ug, axes=(1, 3), norm='ortho').real.astype(u.dtype)
    return jnp.einsum('bngc,gcd->bngd', f, w_grp).reshape(b, n, BRANCH_W)


def pool_mix(u, w_grp, scale):
    b, n, _ = u.shape
    u32 = u.astype(jnp.float32)
    cs = jnp.concatenate([jnp.zeros((b, 1, BRANCH_W), jnp.float32), jnp.cumsum(u32, axis=1)], axis=1)
    t = jnp.arange(n)
    means = []
    for g, w in enumerate(POOL_WINDOWS):
        lo = jnp.clip(t - w // 2, 0, n)
        hi = jnp.clip(t - w // 2 + w, 0, n)
        seg = cs[:, :, g * POOL_GW:(g + 1) * POOL_GW]
        means.append((seg[:, hi] - seg[:, lo]) / (hi - lo).astype(jnp.float32)[:, None])
    pooled = (jnp.concatenate(means, axis=-1) - u32).astype(u.dtype)
    pooled = pooled.reshape(b, n, len(POOL_WINDOWS), POOL_GW)
    y = jnp.einsum('bngc,gcd->bngd', pooled, w_grp).reshape(b, n, BRANCH_W)
    return y * scale


def diff_lambda(lam_vecs, lambda_init):
    lv = lam_vecs.astype(jnp.float32)
    return jnp.exp(jnp.sum(lv[0] * lv[1])) - jnp.exp(jnp.sum(lv[2] * lv[3])) + lambda_init


def diff_weights(q, k, lam):
    s = jnp.einsum('bqhcd,bkhcd->bhcqk', q, k).astype(jnp.float32) * (DIFF_HD ** -0.5)
    p = jax.nn.softmax(s, axis=-1)
    return p[:, :, 0] - lam * p[:, :, 1]


def diff_out(a, v, subln_w, lambda_init):
    o = jnp.einsum('bhqk,bkhe->bqhe', a.astype(v.dtype), v).astype(jnp.float32)
    o = o * lax.rsqrt(jnp.mean(jnp.square(o), axis=-1, keepdims=True) + SUBLN_EPS)
    o = o * subln_w.astype(jnp.float32) * (1.0 - lambda_init)
    return o.astype(v.dtype).reshape(o.shape[:2] + (BRANCH_W,))


def diff_attn_latent(q, k_all, v_all, lam, subln_w, lambda_init):
    b, n = q.shape[:2]
    nb = n // Q_BLOCK
    qb = jnp.moveaxis(q.reshape((b, nb, Q_BLOCK) + q.shape[2:]), 1, 0)

    def block(qi):
        return diff_out(diff_weights(qi, k_all, lam), v_all, subln_w, lambda_init)

    o = lax.map(block, qb)
    return jnp.moveaxis(o, 0, 1).reshape(b, n, BRANCH_W)


def softmax_attend(q, k, v):
    s = jnp.einsum('bqhd,bkhd->bhqk', q, k).astype(jnp.float32) * (q.shape[-1] ** -0.5)
    p = jax.nn.softmax(s, axis=-1).astype(v.dtype)
    o = jnp.einsum('bhqk,bkhd->bqhd', p, v)
    return o.reshape(o.shape[:2] + (-1,))


def na_latent(q, k, v, k_ctx, v_ctx, bias_tab):
    b, n = q.shape[:2]
    rows = n // GRID_W
    kh = min(NA_KH, rows)
    kw = min(NA_KW, GRID_W)
    qg = q.reshape(b, rows, GRID_W, NA_HEADS, NA_HD)
    kg = k.reshape(b, rows, GRID_W, NA_HEADS, NA_HD)
    vg = v.reshape(b, rows, GRID_W, NA_HEADS, NA_HD)
    col = jnp.arange(GRID_W)
    cstart = jnp.clip(col - kw // 2, 0, GRID_W - kw)
    col_idx = cstart[:, None] + jnp.arange(kw)[None, :]
    col_off = col_idx - col[:, None] + (NA_KW - 1)
    scale = NA_HD ** -0.5

    def row_block(args):
        r, q_r = args
        rs = jnp.clip(r - kh // 2, 0, rows - kh)
        k_win = lax.dynamic_slice_in_dim(kg, rs, kh, axis=1)[:, :, col_idx]
        v_win = lax.dynamic_slice_in_dim(vg, rs, kh, axis=1)[:, :, col_idx]
        row_off = rs + jnp.arange(kh) - r + (NA_KH - 1)
        bias = bias_tab[:, row_off[None, :, None], col_off[:, None, :]]
        s_win = jnp.einsum('bqhd,bkqjhd->bhqkj', q_r, k_win).astype(jnp.float32) * scale
        s_win = s_win + bias.astype(jnp.float32)[None]
        s_ctx = jnp.einsum('bqhd,bkhd->bhqk', q_r, k_ctx).astype(jnp.float32) * scale
        logits = jnp.concatenate([s_win.reshape(b, NA_HEADS, GRID_W, kh * kw), s_ctx], axis=-1)
        p = jax.nn.softmax(logits, axis=-1).astype(v.dtype)
        p_win = p[..., :kh * kw].reshape(b, NA_HEADS, GRID_W, kh, kw)
        return (jnp.einsum('bhqkj,bkqjhd->bqhd', p_win, v_win)
                + jnp.einsum('bhqk,bkhd->bqhd', p[..., kh * kw:], v_ctx))

    o = lax.map(row_block, (jnp.arange(rows), jnp.moveaxis(qg, 1, 0)))
    return jnp.moveaxis(o, 0, 1).reshape(b, n, BRANCH_W)


def gate_and_merge(z, ys, w_branch, w_out):
    gated = [y * jax.nn.silu(z[i]) for y, i in zip(ys, (1, 5, 7, 11))]
    b, n, _ = z[12].shape
    g = jax.nn.sigmoid(z[12].reshape(b, n, N_BRANCH, D_MODEL).astype(jnp.float32)).astype(z[12].dtype)
    proj = jnp.einsum('bnie,ied->bnid', jnp.stack(gated, axis=2), w_branch)
    return jnp.einsum('bnd,de->bne', jnp.sum(g * proj, axis=2), w_out)


def trunk_layer(x, ctx, c, c_ctx, w_mod, b_mod, w_in, b_in, fnet_w, diff_lam, diff_subln,
                pool_w, pool_scale, na_bias, w_branch, w_out, ln_g, ln_b, lambda_init, alpha, need_ctx):
    n = x.shape[1]
    mod_lat = jnp.dot(jax.nn.silu(c), w_mod) + b_mod
    mod_ctx = jnp.dot(jax.nn.silu(c_ctx), w_mod) + b_mod
    sh_l, sc_l, g_l = jnp.split(mod_lat[:, None, :], 3, axis=-1)
    sh_c, sc_c, g_c = jnp.split(mod_ctx, 3, axis=-1)
    z_lat = split_proj(jnp.dot(layer_norm(x) * (1.0 + sc_l) + sh_l, w_in) + b_in)
    z_ctx = split_proj(jnp.dot(layer_norm(ctx) * (1.0 + sc_c) + sh_c, w_in) + b_in)
    lam = diff_lambda(diff_lam, lambda_init)
    ang = axial_rope_angles(n)
    dk_c = to_heads(z_ctx[3], (DIFF_HEADS, 2, DIFF_HD))
    dv_c = to_heads(z_ctx[4], (DIFF_HEADS, DIFF_VD))
    nk_c = to_heads(z_ctx[9], (NA_HEADS, NA_HD))
    nv_c = to_heads(z_ctx[10], (NA_HEADS, NA_HD))
    dq_l = apply_rope(to_heads(z_lat[2], (DIFF_HEADS, 2, DIFF_HD)), ang)
    dk_l = apply_rope(to_heads(z_lat[3], (DIFF_HEADS, 2, DIFF_HD)), ang)
    dv_l = to_heads(z_lat[4], (DIFF_HEADS, DIFF_VD))
    y_f = fourier_mix(z_lat[0], fnet_w)
    y_d = diff_attn_latent(dq_l, jnp.concatenate([dk_c, dk_l], axis=1),
                           jnp.concatenate([dv_c, dv_l], axis=1), lam, diff_subln, lambda_init)
    y_p = pool_mix(z_lat[6], pool_w, pool_scale)
    y_n = na_latent(to_heads(z_lat[8], (NA_HEADS, NA_HD)), to_heads(z_lat[9], (NA_HEADS, NA_HD)),
                    to_heads(z_lat[10], (NA_HEADS, NA_HD)), nk_c, nv_c, na_bias)
    out_lat = gate_and_merge(z_lat, (y_f, y_d, y_p, y_n), w_branch, w_out)
    x_new = layer_norm(alpha * x + g_l * out_lat, ln_g, ln_b)
    if not need_ctx:
        return x_new, ctx
    y_fc = fourier_mix(z_ctx[0], fnet_w)
    y_dc = diff_out(diff_weights(to_heads(z_ctx[2], (DIFF_HEADS, 2, DIFF_HD)), dk_c, lam),
                    dv_c, diff_subln, lambda_init)
    y_pc = pool_mix(z_ctx[6], pool_w, pool_scale)
    y_nc = softmax_attend(to_heads(z_ctx[8], (NA_HEADS, NA_HD)), nk_c, nv_c)
    out_ctx = gate_and_merge(z_ctx, (y_fc, y_dc, y_pc, y_nc), w_branch, w_out)
    ctx_new = layer_norm(alpha * ctx + g_c * out_ctx, ln_g, ln_b)
    return x_new, ctx_new


def setup_inputs(seed: int = 0) -> dict:
    key = jax.random.key(seed)
    ks = jax.random.split(key, 18)
    beta = (8.0 * DEPTH) ** -0.25

    def nrm(k, shape, s):
        return jax.random.normal(k, shape, jnp.float32) * s

    return {
        'x': nrm(ks[0], (BATCH, SEQ, D_MODEL), 1.0),
        'c': nrm(ks[1], (BATCH, D_MODEL), 1.0),
        'ctx': nrm(ks[2], (BATCH, CTX_LEN, D_MODEL), 1.0),
        'c_ctx': nrm(ks[3], (D_MODEL,), 1.0),
        'w_mod': nrm(ks[4], (DEPTH, D_MODEL, 3 * D_MODEL), 0.5 * D_MODEL ** -0.5),
        'b_mod': nrm(ks[5], (DEPTH, 3 * D_MODEL), 0.01),
        'w_in': nrm(ks[6], (DEPTH, D_MODEL, PROJ_W), D_MODEL ** -0.5),
        'b_in': nrm(ks[7], (DEPTH, PROJ_W), 0.01),
        'fnet_w': nrm(ks[8], (DEPTH, FNET_GROUPS, FNET_GW, FNET_GW), FNET_GW ** -0.5),
        'diff_lam': nrm(ks[9], (DEPTH, 4, DIFF_HD), 0.1),
        'diff_subln': 1.0 + nrm(ks[10], (DEPTH, DIFF_VD), 0.02),
        'pool_w': nrm(ks[11], (DEPTH, len(POOL_WINDOWS), POOL_GW, POOL_GW), POOL_GW ** -0.5),
        'pool_scale': 1.0 + nrm(ks[12], (DEPTH, BRANCH_W), 0.02),
        'na_bias': nrm(ks[13], (DEPTH, NA_HEADS, 2 * NA_KH - 1, 2 * NA_KW - 1), 0.1),
        'w_branch': nrm(ks[14], (DEPTH, N_BRANCH, BRANCH_W, D_MODEL), beta * BRANCH_W ** -0.5),
        'w_out': nrm(ks[15], (DEPTH, D_MODEL, D_MODEL), beta * D_MODEL ** -0.5),
        'ln_g': 1.0 + nrm(ks[16], (DEPTH, D_MODEL), 0.02),
        'ln_b': nrm(ks[17], (DEPTH, D_MODEL), 0.01),
    }


def reference(x, c, ctx, c_ctx, w_mod, b_mod, w_in, b_in, fnet_w, diff_lam, diff_subln,
              pool_w, pool_scale, na_bias, w_branch, w_out, ln_g, ln_b):
    alpha = (2.0 * DEPTH) ** 0.25
    h, hc = x, ctx
    for l in range(DEPTH):
        lambda_init = 0.8 - 0.6 * math.exp(-0.3 * l)
        h, hc = trunk_layer(h, hc, c, c_ctx, w_mod[l], b_mod[l], w_in[l], b_in[l], fnet_w[l],
                            diff_lam[l], diff_subln[l], pool_w[l], pool_scale[l], na_bias[l],
                            w_branch[l], w_out[l], ln_g[l], ln_b[l], lambda_init, alpha,
                            l < DEPTH - 1)
    return h
```

```python
import contextlib
import numpy as np
import concourse.bass as bass
import concourse.mybir as mybir

F32 = mybir.dt.float32
BF16 = mybir.dt.bfloat16
AF = mybir.ActivationFunctionType
ALU = mybir.AluOpType

ENGS = ("pe", "act", "dve", "pool", "sp")
HANDLES = {"pe": "tensor", "act": "scalar", "dve": "vector", "pool": "gpsimd", "sp": "sync"}


class Buf:
    __slots__ = ("name", "writer", "readers")

    def __init__(self, name=""):
        self.name = name
        self.writer = None
        self.readers = []


class Tile:
    def __init__(self, t, name):
        self.t = t
        self.b = Buf(name)

    def __getitem__(self, k):
        return self.t[k]


class Op:
    __slots__ = ("eng", "fn", "deps", "ticket", "signal", "is_dma", "slot", "dcount", "prev", "phase", "q", "carry")

    def __init__(self, eng, fn, phase, is_dma=False):
        self.eng = eng
        self.fn = fn
        self.deps = []
        self.ticket = None
        self.signal = False
        self.is_dma = is_dma
        self.slot = None
        self.dcount = None
        self.prev = None
        self.phase = phase
        self.carry = False


class Prog:
    def __init__(self, nc, nslots=None):
        self.nc = nc
        self.nslots = nslots or {"sp": 16, "pool": 16, "act": 8, "cc": 1}
        self.qinc = {"sp": 16, "pool": 16, "act": 16, "cc": 1}
        self.qeng = {"sp": "sp", "pool": "pool", "act": "act", "cc": "pool"}
        self.gstack = contextlib.ExitStack()
        self.esem = {e: self.gstack.enter_context(nc.semaphore("s_" + e)) for e in ENGS}
        self.dsem = {q: [self.gstack.enter_context(nc.semaphore("d_%s%d" % (q, i))) for i in range(n)]
                     for q, n in self.nslots.items()}
        self.slot_rr = {q: 0 for q in self.nslots}
        self.slot_last = {q: [None] * n for q, n in self.nslots.items()}
        self.slot_count = {q: [0] * n for q, n in self.nslots.items()}
        self.ecount = {e: 0 for e in ENGS}
        self.waited = {e: {} for e in ENGS}
        self.phase = 0
        self.pstack = None
        self.ops = {e: [] for e in ENGS}
        self.uid = 0
        self.nblocks = 0
        self.clear_all()

    def clear_all(self):
        sems = list(self.esem.values()) + [s for q in self.dsem.values() for s in q]
        with self.nc.Block() as block:
            def body(eng):
                for s in sems:
                    eng.sem_clear(s)
            block.gpsimd(body)

    def _nm(self, name):
        self.uid += 1
        return "%s_%d" % (name, self.uid)

    def gtile(self, name, shape, dtype=F32):
        return Tile(self.gstack.enter_context(self.nc.sbuf_tensor(self._nm(name), list(shape), dtype)), name)

    def tile(self, name, shape, dtype=F32):
        return Tile(self.pstack.enter_context(self.nc.sbuf_tensor(self._nm(name), list(shape), dtype)), name)

    def ptile(self, name, shape, dtype=F32):
        return Tile(self.pstack.enter_context(self.nc.psum_tensor(self._nm(name), list(shape), dtype)), name)

    def begin(self):
        assert self.pstack is None
        self.pstack = contextlib.ExitStack()
        self.phase += 1
        self.ops = {e: [] for e in ENGS}

    def end(self):
        self._emit()
        self.pstack.close()
        self.pstack = None

    def finish(self):
        self.clear_all()
        self.gstack.close()

    def _add_deps(self, op, reads, writes):
        deps = []
        for b in reads:
            if b.writer is not None:
                deps.append(b.writer)
        for b in writes:
            if b.writer is not None:
                deps.append(b.writer)
            deps.extend(b.readers)
        seen = set()
        for d in deps:
            if d is op or id(d) in seen or (d.phase != self.phase and not d.carry):
                continue
            seen.add(id(d))
            if d.eng == "pe" and op.eng == "pe" and not d.is_dma and not op.is_dma:
                continue
            op.deps.append(d)
            d.signal = True
        for b in reads:
            b.readers.append(op)
        for b in writes:
            b.writer = op
            b.readers = []

    def op(self, eng, fn, reads=(), writes=()):
        o = Op(eng, fn, self.phase)
        self._add_deps(o, reads, writes)
        self.ops[eng].append(o)
        return o

    def dma(self, q, out, in_, reads=(), writes=(), fn=None, carry=False, **kw):
        o = Op(self.qeng[q], None, self.phase, is_dma=True)
        o.q = q
        o.carry = carry
        o.fn = fn if fn is not None else (lambda e: e.dma_start(out=out, in_=in_, **kw))
        n = self.nslots[q]
        s = self.slot_rr[q]
        self.slot_rr[q] = (s + 1) % n
        o.slot = s
        o.prev = self.slot_last[q][s]
        self.slot_count[q][s] += self.qinc[q]
        o.dcount = self.slot_count[q][s]
        self.slot_last[q][s] = o
        o.signal = True
        self._add_deps(o, reads, writes)
        self.ops[self.qeng[q]].append(o)
        return o

    def _emit(self):
        nc = self.nc
        for e in ENGS:
            c = self.ecount[e]
            for o in self.ops[e]:
                if o.is_dma:
                    continue
                if o.signal:
                    c += 1
                    o.ticket = c
            self.ecount[e] = c
        if not any(self.ops[e] for e in ENGS):
            return
        self.nblocks += 1
        with nc.Block() as block:
            for e in ENGS:
                ops = self.ops[e]
                if not ops:
                    continue

                def body(eng, ops=ops, e=e):
                    waited = self.waited[e]

                    def wait(key, sem, val):
                        if waited.get(key, 0) >= val:
                            return
                        waited[key] = val
                        eng.wait_ge(sem, val)

                    for o in ops:
                        for d in o.deps:
                            if d.is_dma:
                                wait(("d", d.q, d.slot), self.dsem[d.q][d.slot], d.dcount)
                            else:
                                wait(("e", d.eng), self.esem[d.eng], d.ticket)
                        if o.is_dma:
                            p = o.prev
                            if p is not None and (p.phase == o.phase or p.carry):
                                wait(("d", p.q, p.slot), self.dsem[p.q][p.slot], p.dcount)
                            if o.q == "cc":
                                o.fn(eng).then_inc(self.dsem[o.q][o.slot])
                            else:
                                o.fn(eng).then_inc(self.dsem[o.q][o.slot], 16)
                        else:
                            ins = o.fn(eng)
                            if o.signal:
                                ins.then_inc(self.esem[e], 1)
                    for q in self.nslots:
                        if self.qeng[q] != e:
                            continue
                        for s, last in enumerate(self.slot_last[q]):
                            if last is not None and last.phase == self.phase and not last.carry:
                                wait(("d", q, s), self.dsem[q][s], last.dcount)

                getattr(block, HANDLES[e])(body)
from concourse.bass_utils import run_bass_kernel_spmd

import math
import ml_dtypes

D = 1024
NF = 4352
NQ = 2304
NKN = 2816
ALPHA = (2.0 * 4) ** 0.25
AX = mybir.AxisListType.X


def build_program(n_layers=1, debug=False):
    nc = bass.Bass("TRN2", target_bir_lowering=False)
    L = n_layers

    def din(name, shape, dt=F32):
        return nc.dram_tensor(name, list(shape), dt, kind="ExternalInput").ap()

    def dscr(name, shape, dt=BF16):
        return nc.dram_tensor(name, list(shape), dt, kind="ExternalOutput" if debug else "Internal").ap()

    hx = din("hx", [4096, D])
    hc = din("hc", [256, D])
    cvec = din("cvec", [16, 128])
    w_mod = din("w_mod", [L, D, 3 * D])
    b_mod = din("b_mod", [L, 3 * D])
    w_in = din("w_in", [L, D, 10240])
    b_in = din("b_in", [L, 10240])
    fnet_w = din("fnet_w", [L, 4, 128, 128])
    diff_lam = din("diff_lam", [L, 256])
    diff_subln = din("diff_subln", [L, 128])
    pool_w = din("pool_w", [L, 4, 128, 128])
    pool_scale = din("pool_scale", [L, 512])
    w_branch = din("w_branch", [L, 4, 512, D])
    w_out = din("w_out", [L, D, D])
    ln_g = din("ln_g", [L, D])
    ln_b = din("ln_b", [L, D])
    nab = din("nab", [L, 8, 128, 3840])
    lamc = din("lamc", [L, 2])
    dftc = din("dftc", [4096, 2048], BF16)
    dfts = din("dfts", [4096, 2048], BF16)
    dc256 = din("dc256", [256, 256], BF16)
    ds256 = din("ds256", [256, 256], BF16)
    ccc = din("ccc", [128, 128], BF16)
    nscc = din("nscc", [128, 128], BF16)
    ropec = din("ropec", [128, 4096])
    ropes = din("ropes", [128, 4096])
    pinv = din("pinv", [4, NQ])
    pmask = din("pmask", [2])
    identf = din("identf", [128, 128])
    hout = nc.dram_tensor("hout", [NQ, D], F32, kind="ExternalOutput").ap()

    u_scr = dscr("u_scr", [NF, 512])
    gate_scr = [dscr("gate_scr%d" % i, [512, NQ]) for i in range(4)]
    dq_scr = dscr("dq_scr", [512, NQ])
    dk_scr = dscr("dk_scr", [512, NF])
    dv_scr = dscr("dv_scr", [NF, 4, 129])
    pin_scr = dscr("pin_scr", [512, 2320], F32)
    nq_scr = dscr("nq_scr", [512, NQ])
    nk_scr = dscr("nk_scr", [512, NKN])
    nv_scr = dscr("nv_scr", [NKN, 8, 65])
    mg_scr = dscr("mg_scr", [4096, NQ])
    G_scr = [dscr("G_scr%d" % i, [512, NQ]) for i in range(4)]

    hown = nc.dram_tensor("hown", [2048, D], F32).ap()
    hctx = nc.dram_tensor("hctx", [256, D], F32).ap()
    agin = nc.dram_tensor("agin", [4096, D], F32).ap()
    agout = nc.dram_tensor("agout", [4096, D], F32).ap()
    PAIRS = [[0, 1], [2, 3], [4, 5], [6, 7]]
    aginb = [Buf("agin%d" % i) for i in range(4)]
    agoutb = [Buf("agout%d" % i) for i in range(4)]

    P = Prog(nc)

    def bl(x):
        return [t.b if hasattr(t, "b") else t for t in x]

    def act(out, in_, func, bias=None, scale=None, accum_out=None, r=(), w=()):
        kw = {}
        if bias is not None:
            kw["bias"] = bias
        if scale is not None:
            kw["scale"] = scale
        if accum_out is not None:
            kw["accum_out"] = accum_out
        P.op("act", lambda e: e.activation(out=out, in_=in_, func=func, **kw), bl(r), bl(w))

    def ts(eng, out, in0, s1, s2, op0, op1=None, r=(), w=()):
        if op1 is None:
            P.op(eng, lambda e: e.tensor_scalar(out=out, in0=in0, scalar1=s1, scalar2=None, op0=op0), bl(r), bl(w))
        else:
            P.op(eng, lambda e: e.tensor_scalar(out=out, in0=in0, scalar1=s1, scalar2=s2, op0=op0, op1=op1), bl(r), bl(w))

    def tt(eng, out, in0, in1, op, r=(), w=()):
        P.op(eng, lambda e: e.tensor_tensor(out=out, in0=in0, in1=in1, op=op), bl(r), bl(w))

    def stt(eng, out, in0, scalar, in1, op0, op1, r=(), w=()):
        P.op(eng, lambda e: e.scalar_tensor_tensor(out=out, in0=in0, scalar=scalar, in1=in1, op0=op0, op1=op1), bl(r), bl(w))

    def cp(eng, out, in_, r=(), w=()):
        P.op(eng, lambda e: e.tensor_copy(out=out, in_=in_), bl(r), bl(w))

    def mm(out, lhsT, rhs, start, stop, r=(), w=(), skip=False):
        P.op("pe", lambda e: e.matmul(out, lhsT, rhs, start=start, stop=stop, skip_group_check=skip), bl(r), bl(w))

    def tr(out, in_, ident, r=(), w=()):
        P.op("pe", lambda e: e.transpose(out, in_, ident), bl(r), bl(w))

    def dma(q, out, in_, r=(), w=(), **kw):
        P.dma(q, out, in_, bl(r), bl(w), **kw)

    def memset(eng, ap, val, w=()):
        P.op(eng, lambda e: e.memset(ap, val), (), bl(w))

    class Rot:
        def __init__(self, tiles):
            self.tiles = tiles
            self.i = 0

        def next(self):
            t = self.tiles[self.i % len(self.tiles)]
            self.i += 1
            return t

    class View:
        def __init__(self, ap, name):
            self.ap = ap
            self.b = Buf(name)

        def __getitem__(self, k):
            return self.ap[k]

    def bank_views(name, n, width, dtype):
        t = P.ptile(name, [128, n * width], dtype)
        return Rot([View(t[:, i * width:(i + 1) * width], "%s%d" % (name, i)) for i in range(n)])

    def rot(name, shape, dtype, n, psum=False):
        mk = P.ptile if psum else P.tile
        return Rot([mk("%s%d" % (name, i), shape, dtype) for i in range(n)])

    identF = P.gtile("identF", [128, 128], F32)
    identB = P.gtile("identB", [128, 128], BF16)
    ones = P.gtile("ones", [128, 128], F32)
    bfm = P.gtile("bfm", [128, 116], F32)
    modfm = P.gtile("modfm", [128, 24, 2], F32)
    opsc = P.gtile("opsc", [128, 8, 2], F32)
    grep = [P.gtile("grep%d" % r, [128, D], F32) for r in range(2)]
    lng = P.gtile("lng", [128, D], F32)
    lnb = P.gtile("lnb", [128, D], F32)
    bu = P.gtile("bu", [128, 512], F32)
    bdv = P.gtile("bdv", [128, 512], F32)
    bnv = P.gtile("bnv", [128, 512], F32)
    subl = P.gtile("subl", [128, 128], F32)
    lamt = P.gtile("lamt", [128, 8], F32)
    pmk = P.gtile("pmk", [128, 2], F32)
    cccT = P.gtile("cccT", [128, 128], BF16)
    nsccT = P.gtile("nsccT", [128, 128], BF16)
    fwT = P.gtile("fwT", [128, 4, 128], BF16)
    pwT = P.gtile("pwT", [128, 4, 128], BF16)

    for l in range(L):
        P.begin()
        if l == 0:
            dma("sp", identF[:], identf, w=[identF])
            dma("pool", identB[:], identf, w=[identB])
            memset("dve", ones[:], 1.0, w=[ones])
            dma("sp", cccT[:], ccc, w=[cccT])
            dma("sp", nsccT[:], nscc, w=[nsccT])
            dma("sp", pmk[:], pmask.partition_broadcast(128), w=[pmk])
        dma("pool", fwT[:], fnet_w[l].rearrange("g c d -> c g d"), w=[fwT])
        dma("pool", pwT[:], pool_w[l].rearrange("g c d -> c g d"), w=[pwT])
        sr = P.tile("sr", [128, 128], F32)
        memset("dve", sr[:], 0.0, w=[sr])
        dma("sp", sr[0:80, :], b_in[l].rearrange("(j p) -> j p", p=128), r=[sr], w=[sr])
        bsrc = b_in[l, 1024:2048].rearrange("(j g s f) -> j g s f", g=4, s=2, f=16)
        bdst = sr[80:88, :].rearrange("j (g s f) -> j g s f", g=4, s=2, f=16)
        dma("sp", bdst[:, :, 0, :], bsrc[:, :, 1, :], r=[sr], w=[sr])
        dma("sp", bdst[:, :, 1, :], bsrc[:, :, 0, :], r=[sr], w=[sr])
        dma("sp", sr[88:112, :], b_mod[l].rearrange("(j p) -> j p", p=128), r=[sr], w=[sr])
        dma("sp", sr[112:116, :], pool_scale[l].rearrange("(j p) -> j p", p=128), r=[sr], w=[sr])
        sr2 = P.tile("sr2", [16, 128], F32)
        dma("sp", sr2[:], cvec, w=[sr2])
        pz = P.ptile("pz", [128, 512], F32)
        tr(pz[:, 0:128], sr[:], identF[:], r=[sr, identF], w=[pz])
        cp("dve", bfm[:], pz[:, 0:116], r=[pz], w=[bfm])
        tr(pz[:, 128:144], sr2[:], identF[0:16, 0:16], r=[sr2, identF], w=[pz])
        csil = P.tile("csil", [128, 16], F32)
        act(csil[:], pz[:, 128:144], AF.Silu, r=[pz], w=[csil])
        crep = [P.tile("crep%d" % r, [128, 8, 128], F32) for r in range(2)]
        for r_ in range(2):
            for kc in range(8):
                ts("dve", crep[r_][:, kc, :], ones[:], csil[:, r_ * 8 + kc:r_ * 8 + kc + 1], None, ALU.mult,
                   r=[ones, csil], w=[crep[r_]])
        brep = P.tile("brep", [128, D], F32)
        dma("sp", brep[:], b_mod[l, 2048:3072].partition_broadcast(128), w=[brep])
        dma("sp", lng[:], ln_g[l].partition_broadcast(128), w=[lng])
        dma("sp", lnb[:], ln_b[l].partition_broadcast(128), w=[lnb])
        dma("sp", bu[:], b_in[l, 0:512].partition_broadcast(128), w=[bu])
        dma("sp", bdv[:], b_in[l, 2048:2560].partition_broadcast(128), w=[bdv])
        dma("sp", bnv[:], b_in[l, 5120:5632].partition_broadcast(128), w=[bnv])
        dma("sp", subl[:], diff_subln[l].partition_broadcast(128), w=[subl])
        dlam = P.tile("dlam", [128, 256], F32)
        dma("sp", dlam[:], diff_lam[l].partition_broadcast(128), w=[dlam])
        dma("sp", lamt[:, 0:2], lamc[l].partition_broadcast(128), w=[lamt])
        ltmp = P.tile("ltmp", [128, 128], F32)
        tt("dve", ltmp[:, 0:64], dlam[:, 0:64], dlam[:, 64:128], ALU.mult, r=[dlam], w=[ltmp])
        tt("dve", ltmp[:, 64:128], dlam[:, 128:192], dlam[:, 192:256], ALU.mult, r=[dlam, ltmp], w=[ltmp])
        P.op("dve", lambda e: e.tensor_reduce(out=lamt[:, 2:4], in_=ltmp[:].rearrange("p (a b) -> p a b", a=2), axis=AX, op=ALU.add),
             bl([ltmp, lamt]), bl([lamt]))
        act(lamt[:, 2:4], lamt[:, 2:4], AF.Exp, r=[lamt], w=[lamt])
        tt("dve", lamt[:, 4:5], lamt[:, 2:3], lamt[:, 3:4], ALU.subtract, r=[lamt], w=[lamt])
        tt("dve", lamt[:, 4:5], lamt[:, 4:5], lamt[:, 0:1], ALU.add, r=[lamt], w=[lamt])
        ts("dve", lamt[:, 5:6], lamt[:, 4:5], -1.0, None, ALU.mult, r=[lamt], w=[lamt])
        ts("dve", subl[:], subl[:], lamt[:, 1:2], None, ALU.mult, r=[subl, lamt], w=[subl])
        wm = rot("wm", [128, 8, 512], F32, 2)
        pg = [P.ptile("pg%d" % i, [128, 512], F32) for i in range(4)]
        wmr = w_mod[l].rearrange("(k p) n -> p k n", p=128)
        for pc in range(6):
            wt = wm.next()
            dma("sp" if pc % 2 == 0 else "act", wt[:], wmr[:, :, pc * 512:(pc + 1) * 512], w=[wt])
            for jj in range(4):
                j = pc * 4 + jj
                for kc in range(8):
                    mm(pz[:, 256 + 2 * j:256 + 2 * j + 2], wt[:, kc, jj * 128:(jj + 1) * 128],
                       csil[:].rearrange("p (r k) -> p k r", r=2)[:, kc, :], kc == 0, kc == 7,
                       r=[wt, csil], w=[pz])
            if pc >= 4:
                for r_ in range(2):
                    pgt = pg[(pc - 4) * 2 + r_]
                    for kc in range(8):
                        mm(pgt[:], crep[r_][:, kc, :], wt[:, kc, :], kc == 0, kc == 7, r=[crep[r_], wt], w=[pgt])
                    hh = pc - 4
                    tt("dve", grep[r_][:, hh * 512:(hh + 1) * 512], pgt[:], brep[:, hh * 512:(hh + 1) * 512], ALU.add,
                       r=[pgt, brep], w=[grep[r_]])
        for r_ in range(2):
            tt("dve", modfm[:, :, r_], pz[:, 256:304].rearrange("p (j r) -> p j r", r=2)[:, :, r_], bfm[:, 88:112], ALU.add,
               r=[pz, bfm], w=[modfm])
        ts("dve", opsc[:], modfm[:, 8:16, :], 1.0, None, ALU.add, r=[modfm], w=[opsc])
        P.end()

        P.begin()
        xT = P.tile("xT", [128, 8, NF], BF16)
        xTb = [Buf("xT%d" % i) for i in range(34)]

        def xtr(s0, n):
            return [xTb[t] for t in range(s0 // 128, (s0 + n - 1) // 128 + 1)]
        xin = rot("xin", [128, D], F32, 2)
        xin2 = rot("xin2", [128, D], F32, 2)
        yln = rot("yln", [128, D], F32, 2)
        stt_ = rot("st", [128, 16], F32, 2)
        ptr = rot("ptr", [128, 512], F32, 2, psum=True)
        for tI in range(34):
            r_ = 0 if tI < 32 else 1
            xt_ = xin.next()
            yt_ = yln.next()
            s_ = stt_.next()
            if l == 0:
                src = hx[tI * 128:(tI + 1) * 128, :] if tI < 32 else hc[(tI - 32) * 128:(tI - 31) * 128, :]
                dma("sp", xt_[:], src, w=[xt_])
            elif tI < 16:
                dma("sp", xt_[:], hown[tI * 128:(tI + 1) * 128, :], w=[xt_])
            elif tI >= 32:
                dma("sp", xt_[:], hctx[(tI - 32) * 128:(tI - 31) * 128, :], w=[xt_])
            else:
                jj = tI - 16
                xb_ = xin2.next()
                dma("sp", xt_[:], agout[jj * 128:(jj + 1) * 128, :], r=[agoutb[jj // 8]], w=[xt_])
                dma("act", xb_[:], agout[2048 + jj * 128:2048 + (jj + 1) * 128, :], r=[agoutb[2 + jj // 8]], w=[xb_])
                act(xt_[:], xt_[:], AF.Identity, scale=pmk[:, 0:1], r=[xt_, pmk], w=[xt_])
                stt("dve", xt_[:], xb_[:], pmk[:, 1:2], xt_[:], ALU.mult, ALU.add, r=[xb_, pmk, xt_], w=[xt_])
            for hh in range(2):
                P.op("dve", lambda e, s_=s_, xt_=xt_, hh=hh: e.bn_stats(out=s_[:, hh * 6:(hh + 1) * 6], in_=xt_[:, hh * 512:(hh + 1) * 512]),
                     bl([xt_, s_]), bl([s_]))
            P.op("dve", lambda e, s_=s_: e.bn_aggr(out=s_[:, 12:14], in_=s_[:, 0:12]), bl([s_]), bl([s_]))
            ts("dve", s_[:, 14:15], s_[:, 13:14], 1e-6, None, ALU.add, r=[s_], w=[s_])
            act(s_[:, 14:15], s_[:, 14:15], AF.Sqrt, r=[s_], w=[s_])
            P.op("dve", lambda e, s_=s_: e.reciprocal(out=s_[:, 14:15], in_=s_[:, 14:15]), bl([s_]), bl([s_]))
            stt("dve", s_[:, 15:16], s_[:, 12:13], -1.0, s_[:, 14:15], ALU.mult, ALU.mult, r=[s_], w=[s_])
            act(yt_[:], xt_[:], AF.Identity, bias=s_[:, 15:16], scale=s_[:, 14:15], r=[s_, xt_], w=[yt_])
            for hh in range(2):
                pt = ptr.next()
                for k4 in range(4):
                    kc = hh * 4 + k4
                    tr(pt[:, k4 * 128:(k4 + 1) * 128], yt_[:, kc * 128:(kc + 1) * 128], identF[:], r=[yt_, identF], w=[pt])
                for k4 in range(4):
                    kc = hh * 4 + k4
                    if k4 % 2 == 0:
                        act(xT[:, kc, tI * 128:(tI + 1) * 128], pt[:, k4 * 128:(k4 + 1) * 128], AF.Identity,
                            bias=modfm[:, kc, r_:r_ + 1], scale=opsc[:, kc, r_:r_ + 1], r=[pt, modfm, opsc], w=[xTb[tI]])
                    else:
                        ts("dve", xT[:, kc, tI * 128:(tI + 1) * 128], pt[:, k4 * 128:(k4 + 1) * 128],
                           opsc[:, kc, r_:r_ + 1], modfm[:, kc, r_:r_ + 1], ALU.mult, ALU.add, r=[pt, modfm, opsc], w=[xTb[tI]])

        wbf = rot("wbf", [128, 8, 512], BF16, 2)
        wpm = P.tile("wpm", [128, 8, 512], BF16)
        pp = rot("pp", [128, 512], F32, 4, psum=True)
        ofm = rot("ofm", [128, NF], BF16, 2)
        ofm32 = P.tile("ofm32", [128, 2320], F32)
        otk = rot("otk", [128, 520], BF16, 3)
        rtab = rot("rtab", [128, 2, 512], F32, 2)
        rt1 = rot("rt1", [128, 512], F32, 2)
        rt2 = rot("rt2", [128, 512], F32, 2)
        orp = rot("orp", [128, 512], BF16, 3)
        w_r = w_in[l].rearrange("(k p) n -> p k n", p=128)
        evac_i = [0]

        def load_w(grp):
            wt = wbf.next()
            dma("pool", wt[:], w_r[:, :, grp * 512:(grp + 1) * 512], w=[wt])
            return wt

        def chunks(ranges):
            out = []
            for (s0, n, d0) in ranges:
                o = 0
                while o < n:
                    c = min(512, n - o)
                    out.append((s0 + o, c, d0 + o))
                    o += c
            return out

        QR = [(0, 2048, 0), (4096, 256, 2048)]
        NKR = [(3840, 256, 0), (0, 2304, 256), (4096, 256, 2560)]
        PLR = [(4088, 8, 0), (0, 2056, 8), (4096, 256, 2064)]

        def fm_group(grp, ranges, dst, mode, ncols):
            wt = load_w(grp)
            for g4 in range(4):
                gcol = grp * 4 + g4
                ot = ofm32 if mode == "pool" else ofm.next()
                for (s0, n, d0) in chunks(ranges):
                    ps = pp.next()
                    for kc in range(8):
                        mm(ps[:, 0:n], wt[:, kc, g4 * 128:(g4 + 1) * 128], xT[:, kc, s0:s0 + n], kc == 0, kc == 7,
                           r=[wt] + xtr(s0, n), w=[ps])
                    bcol = bfm[:, gcol:gcol + 1]
                    if mode == "silu":
                        act(ot[:, d0:d0 + n], ps[:, 0:n], AF.Silu, bias=bcol, r=[ps, bfm], w=[ot])
                    elif mode == "sigmoid":
                        act(ot[:, d0:d0 + n], ps[:, 0:n], AF.Sigmoid, bias=bcol, r=[ps, bfm], w=[ot])
                    elif mode == "q8":
                        ts("dve", ot[:, d0:d0 + n], ps[:, 0:n], bcol, 0.125, ALU.add, ALU.mult, r=[ps, bfm], w=[ot])
                    else:
                        evac_i[0] += 1
                        if evac_i[0] % 2 == 0:
                            act(ot[:, d0:d0 + n], ps[:, 0:n], AF.Identity, bias=bcol, r=[ps, bfm], w=[ot])
                        else:
                            ts("dve", ot[:, d0:d0 + n], ps[:, 0:n], bcol, None, ALU.add, r=[ps, bfm], w=[ot])
                dma("sp", dst[g4 * 128:(g4 + 1) * 128, :], ot[:, 0:ncols], r=[ot])

        def tm_group(grp, tok_tiles, dst_fn, brep_t, nh, hd):
            wt = load_w(grp)
            for i, s0 in enumerate(tok_tiles):
                ps = pp.next()
                for kc in range(8):
                    mm(ps[:], xT[:, kc, s0:s0 + 128], wt[:, kc, :], kc == 0, kc == 7, r=[wt] + xtr(s0, 128), w=[ps])
                ot = otk.next()
                if nh == 1:
                    tt("dve", ot[:, 0:512], ps[:], brep_t[:], ALU.add, r=[ps, brep_t], w=[ot])
                    dma("sp", dst_fn(i), ot[:, 0:512], r=[ot])
                else:
                    ov = ot[:, 0:nh * (hd + 1)].rearrange("p (h e) -> p h e", h=nh)
                    tt("dve", ov[:, :, 0:hd], ps[:].rearrange("p (h e) -> p h e", h=nh),
                       brep_t[:].rearrange("p (h e) -> p h e", h=nh), ALU.add, r=[ps, brep_t], w=[ot])
                    memset("pool", ov[:, :, hd:hd + 1], 1.0, w=[ot])
                    dma("sp", dst_fn(i), ov, r=[ot])

        def rope_group(grp, ranges, dst, boff, swoff):
            wt = load_w(grp)
            wv = wt[:].rearrange("p k (g s f) -> p k g s f", s=2, f=16)
            pv = wpm[:].rearrange("p k (g s f) -> p k g s f", s=2, f=16)
            cp("pool", pv[:, :, :, 0, :], wv[:, :, :, 1, :], r=[wt], w=[wpm])
            cp("pool", pv[:, :, :, 1, :], wv[:, :, :, 0, :], r=[wt, wpm], w=[wpm])
            for (s0, n, d0) in chunks(ranges):
                isctx = s0 >= 4096
                if not isctx:
                    rtb = rtab.next()
                    dma("act", rtb[:, 0, 0:n], ropec[:, s0:s0 + n], w=[rtb])
                    dma("act", rtb[:, 1, 0:n], ropes[:, s0:s0 + n], r=[rtb], w=[rtb])
                for g4 in range(4):
                    psa = pp.next()
                    for kc in range(8):
                        mm(psa[:, 0:n], wt[:, kc, g4 * 128:(g4 + 1) * 128], xT[:, kc, s0:s0 + n], kc == 0, kc == 7,
                           r=[wt] + xtr(s0, n), w=[psa])
                    ot = orp.next()
                    bcol = bfm[:, boff + g4:boff + g4 + 1]
                    if isctx:
                        act(ot[:, 0:n], psa[:, 0:n], AF.Identity, bias=bcol, r=[psa, bfm], w=[ot])
                    else:
                        psb = pp.next()
                        for kc in range(8):
                            mm(psb[:, 0:n], wpm[:, kc, g4 * 128:(g4 + 1) * 128], xT[:, kc, s0:s0 + n], kc == 0, kc == 7,
                               r=[wpm] + xtr(s0, n), w=[psb])
                        t1 = rt1.next()
                        t2 = rt2.next()
                        stt("dve", t1[:, 0:n], psa[:, 0:n], bcol, rtb[:, 0, 0:n], ALU.add, ALU.mult, r=[psa, bfm, rtb], w=[t1])
                        stt("dve", t2[:, 0:n], psb[:, 0:n], bfm[:, swoff + g4:swoff + g4 + 1], rtb[:, 1, 0:n], ALU.add, ALU.mult,
                            r=[psb, bfm, rtb], w=[t2])
                        tt("pool", ot[:, 0:n], t1[:, 0:n], t2[:, 0:n], ALU.add, r=[t1, t2], w=[ot])
                    dma("sp", dst[g4 * 128:(g4 + 1) * 128, d0:d0 + n], ot[:, 0:n], r=[ot])

        allt = [i * 128 for i in range(34)]
        nkt = [3840, 3968] + [i * 128 for i in range(18)] + [4096, 4224]
        tm_group(0, allt, lambda i: u_scr[i * 128:(i + 1) * 128, :], bu, 1, 512)
        fm_group(1, QR, gate_scr[0], "silu", NQ)
        rope_group(2, QR, dq_scr, 8, 80)
        rope_group(3, [(0, 4096, 0), (4096, 256, 4096)], dk_scr, 12, 84)
        tm_group(4, allt, lambda i: dv_scr[i * 128:(i + 1) * 128, :, :], bdv, 4, 128)
        fm_group(5, QR, gate_scr[1], "silu", NQ)
        fm_group(6, PLR, pin_scr, "pool", 2320)
        fm_group(7, QR, gate_scr[2], "silu", NQ)
        fm_group(8, QR, nq_scr, "q8", NQ)
        fm_group(9, NKR, nk_scr, "plain", NKN)
        tm_group(10, nkt, lambda i: nv_scr[i * 128:(i + 1) * 128, :, :], bnv, 8, 64)
        fm_group(11, QR, gate_scr[3], "silu", NQ)
        for g in range(8):
            fm_group(12 + g, QR, mg_scr[g * 512:(g + 1) * 512, :], "sigmoid", NQ)
        P.end()

        if debug == "s2":
            break

        P.begin()
        usb = P.tile("usb", [128, 34, 512], BF16)
        for t0 in range(0, 34, 6):
            t1_ = min(34, t0 + 6)
            dma("sp" if (t0 // 6) % 2 == 0 else "act", usb[:, t0:t1_, :],
                u_scr.rearrange("(t p) c -> p t c", p=128)[:, t0:t1_, :], r=[usb], w=[usb])
        dcb = rot("dcb", [128, 8, 256], BF16, 2)
        dsb = rot("dsb", [128, 8, 256], BF16, 2)
        pacc = [P.ptile("pacc%d" % i, [128, 512], F32) for i in range(4)]
        pch = rot("pch", [128, 256], F32, 2, psum=True)
        pw_ = rot("pw", [128, 256], F32, 2, psum=True)
        ab = rot("ab", [128, 2, 256], BF16, 3)
        fb = rot("fb", [128, 256], BF16, 2)
        gtl = rot("gtl", [128, 4, 256], BF16, 2)
        gout = rot("gout", [128, 4, 256], BF16, 2)
        dcr = dftc.rearrange("(t p) k -> p t k", p=128)
        dsr = dfts.rearrange("(t p) k -> p t k", p=128)
        dc2r = dc256.rearrange("(t p) k -> p t k", p=128)
        ds2r = ds256.rearrange("(t p) k -> p t k", p=128)
        for kq in range(9):
            if debug == "s3a" and kq != 8:
                continue
            if debug == "s3b" and kq >= 2:
                continue
            isctx = kq == 8
            ntl = 2 if isctx else 32
            tbase = 32 if isctx else 0
            qcol = 2048 if isctx else kq * 256
            gt_ = gtl.next()
            dma("act", gt_[:], gate_scr[0].rearrange("(g c) q -> c g q", c=128)[:, :, qcol:qcol + 256], w=[gt_])
            for nt in range(ntl):
                if nt % 8 == 0:
                    dct = dcb.next()
                    dst_ = dsb.next()
                    if isctx:
                        dma("sp", dct[:, 0:2, :], dc2r, w=[dct])
                        dma("pool", dst_[:, 0:2, :], ds2r, w=[dst_])
                    else:
                        dma("sp", dct[:], dcr[:, nt:nt + 8, kq * 256:(kq + 1) * 256], w=[dct])
                        dma("pool", dst_[:], dsr[:, nt:nt + 8, kq * 256:(kq + 1) * 256], w=[dst_])
                for g in range(4):
                    lt = usb[:, tbase + nt, g * 128:(g + 1) * 128]
                    mm(pacc[g][:, 0:256], lt, dct[:, nt % 8, :], nt == 0, nt == ntl - 1, r=[usb, dct], w=[pacc[g]], skip=True)
                    mm(pacc[g][:, 256:512], lt, dst_[:, nt % 8, :], False, nt == ntl - 1, r=[usb, dst_], w=[pacc[g]], skip=True)
            go = gout.next()
            for g in range(4):
                a_ = ab.next()
                cp("dve", a_[:].rearrange("p a b -> p (a b)"), pacc[g][:], r=[pacc[g]], w=[a_])
                pc_ = pch.next()
                mm(pc_[:], cccT[:], a_[:, 0, :], True, False, r=[cccT, a_], w=[pc_])
                mm(pc_[:], nsccT[:], a_[:, 1, :], False, True, r=[nsccT, a_], w=[pc_])
                f_ = fb.next()
                act(f_[:], pc_[:], AF.Identity, r=[pc_], w=[f_])
                pw2 = pw_.next()
                mm(pw2[:], fwT[:, g, :], f_[:], True, True, r=[fwT, f_], w=[pw2])
                tt("dve", go[:, g, :], pw2[:], gt_[:, g, :], ALU.mult, r=[pw2, gt_], w=[go])
            dma("sp", G_scr[0].rearrange("(g c) q -> c g q", c=128)[:, :, qcol:qcol + 256], go[:], r=[go])
        P.end()

        if debug in ("s3", "s3a", "s3b"):
            break
        P.begin()
        pbuf = rot("pbuf", [128, 2320], F32, 2)
        pa = P.tile("pa", [128, 2064], F32)
        pb_ = P.tile("pb", [128, 2064], F32)
        pcx = P.tile("pcx", [128, 272], F32)
        pcxa = P.tile("pcxa", [128, 272], F32)
        pcxb = P.tile("pcxb", [128, 272], F32)
        pinvt = rot("pinvt", [128, NQ], F32, 2)
        pooled = rot("pooled", [128, NQ], BF16, 2)
        pgt = rot("pgt", [128, NQ], BF16, 2)
        pgo = rot("pgo", [128, NQ], BF16, 2)
        pps = rot("pps", [128, 512], F32, 2, psum=True)
        memset("dve", pcx[:], 0.0, w=[pcx])

        def wsum(eng, src, n, g, A, Bt):
            tt(eng, A[:, 1:n + 16], src[:, 0:n + 15], src[:, 1:n + 16], ALU.add, r=[src_t[0], A], w=[A])
            if g == 0:
                return A
            tt(eng, Bt[:, 2:n + 15], A[:, 1:n + 14], A[:, 3:n + 16], ALU.add, r=[A, Bt], w=[Bt])
            if g == 1:
                return Bt
            tt(eng, A[:, 4:n + 13], Bt[:, 2:n + 11], Bt[:, 6:n + 15], ALU.add, r=[Bt, A], w=[A])
            if g == 2:
                return A
            tt(eng, Bt[:, 8:n + 9], A[:, 4:n + 5], A[:, 12:n + 13], ALU.add, r=[A, Bt], w=[Bt])
            return Bt

        src_t = [None]
        for g in range(4):
            pt_ = pbuf.next()
            dma("sp", pt_[:], pin_scr[g * 128:(g + 1) * 128, :], w=[pt_])
            pv_ = pinvt.next()
            dma("act", pv_[:], pinv[g].partition_broadcast(128), w=[pv_])
            gt_ = pgt.next()
            dma("pool", gt_[:], gate_scr[2][g * 128:(g + 1) * 128, :], w=[gt_])
            ts("dve", pt_[:, 0:8], pt_[:, 0:8], pmk[:, 0:1], None, ALU.mult, r=[pt_, pmk], w=[pt_])
            ts("dve", pt_[:, 2056:2064], pt_[:, 2056:2064], pmk[:, 1:2], None, ALU.mult, r=[pt_, pmk], w=[pt_])
            pl = pooled.next()
            src_t[0] = pt_
            S_ = wsum("dve", pt_[:, 0:2064], 2048, g, pa, pb_)
            tt("pool", S_[:, 8:2056], S_[:, 8:2056], pv_[:, 0:2048], ALU.mult, r=[S_, pv_], w=[S_])
            tt("pool", pl[:, 0:2048], S_[:, 8:2056], pt_[:, 8:2056], ALU.subtract, r=[S_, pt_], w=[pl])
            cp("dve", pcx[:, 8:264], pt_[:, 2064:2320], r=[pt_, pcx], w=[pcx])
            src_t[0] = pcx
            S2 = wsum("dve", pcx[:, 0:272], 256, g, pcxa, pcxb)
            tt("pool", S2[:, 8:264], S2[:, 8:264], pv_[:, 2048:2304], ALU.mult, r=[S2, pv_], w=[S2])
            tt("pool", pl[:, 2048:2304], S2[:, 8:264], pcx[:, 8:264], ALU.subtract, r=[S2, pcx, pl], w=[pl])
            go = pgo.next()
            for (s0, n, d0) in chunks([(0, NQ, 0)]):
                ps = pps.next()
                mm(ps[:, 0:n], pwT[:, g, :], pl[:, s0:s0 + n], True, True, r=[pwT, pl], w=[ps])
                stt("dve", go[:, s0:s0 + n], ps[:, 0:n], bfm[:, 112 + g:113 + g], gt_[:, s0:s0 + n], ALU.mult, ALU.mult,
                    r=[ps, bfm, gt_], w=[go])
            dma("sp", G_scr[2][g * 128:(g + 1) * 128, :], go[:], r=[go])
        P.end()

        if debug == "s5":
            break
        P.begin()
        kT = rot("kT", [128, NF], BF16, 2)
        vA = rot("vA", [128, 34, 129], BF16, 2)
        qT = rot("qT", [128, NQ], BF16, 2)
        gdt = rot("gdt", [128, NQ], BF16, 2)
        pS = rot("pS", [128, 512], F32, 3, psum=True)
        pO = [P.ptile("pO%d" % i, [128, 512], F32) for i in range(4)]
        pT = rot("pT", [128, 512], BF16, 1, psum=True)
        PT = rot("PT", [128, 512], BF16, 4)
        oc = rot("oc", [128, 2, 4, 129], F32, 2)
        osm = rot("osm", [128, 16], F32, 4)
        o1 = rot("o1", [128, 128], F32, 2)
        o2 = rot("o2", [128, 128], F32, 2)
        onb = rot("onb", [128, 128], BF16, 8)
        sqj = rot("sqj", [128, 128], F32, 2)
        dgo = rot("dgo", [128, 512], BF16, 2)
        dvr = dv_scr.rearrange("(t p) h e -> p t h e", p=128)
        itc = [0]
        dq = []

        def tick():
            itc[0] += 1
            while dq and dq[0][0] <= itc[0]:
                dq.pop(0)[1]()

        def defer(n, fn):
            dq.append((itc[0] + n, fn))
            dq.sort(key=lambda x_: x_[0])

        def flush():
            while dq:
                dq.pop(0)[1]()

        def d_combine(h, q0, nq_, nqt, oc_, gd_):
            go = dgo.next()
            obs = []
            for qi in range(nqt):
                sm = osm.next()
                P.op("dve", lambda e, sm=sm, oc_=oc_, qi=qi: e.reciprocal(out=sm[:, 0:2], in_=oc_[:, :, qi, 128]),
                     bl([oc_, sm]), bl([sm]))
                ts("dve", sm[:, 1:2], sm[:, 1:2], lamt[:, 5:6], None, ALU.mult, r=[sm, lamt], w=[sm])
                a1 = o1.next()
                a2 = o2.next()
                ts("dve", a1[:], oc_[:, 1, qi, 0:128], sm[:, 1:2], None, ALU.mult, r=[oc_, sm], w=[a1])
                stt("dve", a2[:], oc_[:, 0, qi, 0:128], sm[:, 0:1], a1[:], ALU.mult, ALU.add, r=[oc_, sm, a1], w=[a2])
                sq_ = sqj.next()
                tt("pool", sq_[:], a2[:], a2[:], ALU.mult, r=[a2], w=[sq_])
                P.op("dve", lambda e, sm=sm, sq_=sq_: e.tensor_reduce(out=sm[:, 2:3], in_=sq_[:], axis=AX, op=ALU.add),
                     bl([sq_, sm]), bl([sm]))
                ts("dve", sm[:, 3:4], sm[:, 2:3], 1.0 / 128, 1e-5, ALU.mult, ALU.add, r=[sm], w=[sm])
                act(sm[:, 3:4], sm[:, 3:4], AF.Ln, r=[sm], w=[sm])
                act(sm[:, 3:4], sm[:, 3:4], AF.Exp, scale=-0.5, r=[sm], w=[sm])
                ob = onb.next()
                stt("dve", ob[:], a2[:], sm[:, 3:4], subl[:], ALU.mult, ALU.mult, r=[a2, sm, subl], w=[ob])
                obs.append(ob)

            def d_store():
                for qi in range(nqt):
                    ptt = pT.next()
                    tr(ptt[:, 0:128], obs[qi][:], identB[:], r=[obs[qi], identB], w=[ptt])
                    tt("dve", go[:, qi * 128:(qi + 1) * 128], ptt[:, 0:128], gd_[:, q0 + qi * 128:q0 + (qi + 1) * 128], ALU.mult,
                       r=[ptt, gd_], w=[go])
                dma("sp", G_scr[1][h * 128:(h + 1) * 128, q0:q0 + nq_], go[:, 0:nq_], r=[go])
            defer(6, d_store)

        def d_av(p):
            (h, q0, nq_, nqt, comp, ki, kt_i, last, pt_, va_, oc_, gd_) = p
            for qi in range(nqt):
                bank = pO[comp * 2 + qi // 2]
                c0 = (qi % 2) * 129
                first = (ki == 0 and qi % 2 == 0)
                mm(bank[:, c0:c0 + 129], pt_[:, qi * 128:(qi + 1) * 128], va_[:, kt_i, :], first, last,
                   r=[pt_, va_], w=[bank], skip=True)
            if last:
                for bk in range((nqt + 1) // 2):
                    cp("dve", oc_[:, comp, bk * 2:bk * 2 + 2, :].rearrange("p a b -> p (a b)"),
                       pO[comp * 2 + bk][:, 0:258], r=[pO[comp * 2 + bk]], w=[oc_])
                if comp == 1:
                    defer(4, lambda: d_combine(h, q0, nq_, nqt, oc_, gd_))

        pend = None
        for h in range(4):
            kt_ = kT.next()
            va_ = vA.next()
            qt_ = qT.next()
            gd_ = gdt.next()
            dma("sp", kt_[:], dk_scr[h * 128:(h + 1) * 128, :], w=[kt_])
            dma("pool", va_[:], dvr[:, :, h, :], w=[va_])
            dma("act", qt_[:], dq_scr[h * 128:(h + 1) * 128, :], w=[qt_])
            dma("act", gd_[:], gate_scr[1][h * 128:(h + 1) * 128, :], w=[gd_])
            for qc in range(5):
                isctx = qc == 4
                nq_ = 256 if isctx else 512
                nqt = nq_ // 128
                q0 = 2048 if isctx else qc * 512
                ktl = [32, 33] if isctx else list(range(34))
                oc_ = oc.next()
                for comp in range(2):
                    pr = slice(comp * 64, comp * 64 + 64)
                    for ki, kt_i in enumerate(ktl):
                        ps = pS.next()
                        mm(ps[:, 0:nq_], kt_[pr, kt_i * 128:(kt_i + 1) * 128], qt_[pr, q0:q0 + nq_], True, True,
                           r=[kt_, qt_], w=[ps])
                        pt_ = PT.next()
                        act(pt_[:, 0:nq_], ps[:, 0:nq_], AF.Exp, scale=0.125, r=[ps], w=[pt_])
                        if pend is not None:
                            d_av(pend)
                        pend = (h, q0, nq_, nqt, comp, ki, kt_i, ki == len(ktl) - 1, pt_, va_, oc_, gd_)
                        tick()
        d_av(pend)
        flush()
        flush()
        P.end()

        if debug == "s4":
            break
        P.begin()
        nkT = rot("nkT", [64, NKN], BF16, 2)
        nvA = rot("nvA", [128, 22, 65], BF16, 2)
        nqT = rot("nqT", [64, NQ], BF16, 2)
        ngt = rot("ngt", [64, NQ], BF16, 2)
        tbl = rot("tbl", [128, 5, 768], BF16, 2)
        pSa = rot("pSa", [128, 512], F32, 2, psum=True)
        pSb = rot("pSb", [128, 512], F32, 2, psum=True)
        pOn = rot("pOn", [128, 128], F32, 2, psum=True)
        pTn = rot("pTn", [128, 128], BF16, 1, psum=True)
        PTn = rot("PTn", [128, 1024], BF16, 3)
        nsm = rot("nsm", [128, 2], F32, 4)
        nob = rot("nob", [128, 64], BF16, 4)
        ngo = rot("ngo", [64, NQ], BF16, 2)
        nvr = nv_scr.rearrange("(t p) h e -> p t h e", p=128)
        itc = [0]
        dq = []

        def tick():
            itc[0] += 1
            while dq and dq[0][0] <= itc[0]:
                dq.pop(0)[1]()

        def defer(n, fn):
            dq.append((itc[0] + n, fn))
            dq.sort(key=lambda x_: x_[0])

        def flush():
            while dq:
                dq.pop(0)[1]()

        def n_av(p):
            (h, j, ktl, pt_, nv_, ng_, go, lastj) = p
            qcols = slice(j * 128, (j + 1) * 128)
            po = pOn.next()
            for ai, kt_i in enumerate(ktl):
                mm(po[:, 0:65], pt_[:, ai * 128:(ai + 1) * 128], nv_[:, kt_i, :], ai == 0, ai == len(ktl) - 1,
                   r=[pt_, nv_], w=[po])
            sm = nsm.next()
            P.op("dve", lambda e, sm=sm, po=po: e.reciprocal(out=sm[:, 0:1], in_=po[:, 64:65]), bl([po, sm]), bl([sm]))
            ob = nob.next()
            ts("dve", ob[:], po[:, 0:64], sm[:, 0:1], None, ALU.mult, r=[po, sm], w=[ob])

            def n_store():
                ptt = pTn.next()
                tr(ptt[0:64, 0:128], ob[:], identB[:], r=[ob, identB], w=[ptt])
                tt("dve", go[:, qcols], ptt[0:64, 0:128], ng_[:, qcols], ALU.mult, r=[ptt, ng_], w=[go])
                if lastj:
                    dma("sp", G_scr[3][h * 64:(h + 1) * 64, :], go[:], r=[go])
            defer(2, n_store)

        pend = None
        for h in range(8):
            nk_ = nkT.next()
            nv_ = nvA.next()
            nq_t = nqT.next()
            ng_ = ngt.next()
            tb_ = tbl.next()
            dma("sp", nk_[:], nk_scr[h * 64:(h + 1) * 64, :], w=[nk_])
            dma("pool", nv_[:], nvr[:, :, h, :], w=[nv_])
            dma("act", nq_t[:], nq_scr[h * 64:(h + 1) * 64, :], w=[nq_t])
            dma("act", ng_[:], gate_scr[3][h * 64:(h + 1) * 64, :], w=[ng_])
            dma("pool", tb_[:].rearrange("p a b -> p (a b)"), nab[l, h], w=[tb_])
            go = ngo.next()
            for j in range(18):
                isctx = j >= 16
                qcols = slice(j * 128, (j + 1) * 128)
                pat = {0: 1, 1: 2, 14: 3, 15: 4}.get(j, 0)
                pa_ = pSa.next()
                pb2 = pSb.next()
                if isctx:
                    ktl = [20, 21]
                    nwin = 0
                else:
                    alist = list(range(6)) if j == 0 else (list(range(-1, 5)) if j == 15 else list(range(5)))
                    nwin = len(alist)
                    ktl = [j + a for a in alist] + [20, 21]
                for ai, kt_i in enumerate(ktl):
                    bank = pa_ if ai < 4 else pb2
                    c0 = (ai % 4) * 128
                    hasb = ai < nwin
                    mm(bank[:, c0:c0 + 128], nk_[:, kt_i * 128:(kt_i + 1) * 128], nq_t[:, qcols], True, not hasb,
                       r=[nk_, nq_t], w=[bank])
                    if hasb:
                        mm(bank[:, c0:c0 + 128], identB[:], tb_[:, pat, ai * 128:(ai + 1) * 128], False, True,
                           r=[identB, tb_], w=[bank])
                pt_ = PTn.next()
                if isctx:
                    act(pt_[:, 0:256], pa_[:, 0:256], AF.Exp, r=[pa_], w=[pt_])
                else:
                    nb_ = (len(ktl) - 4) * 128
                    act(pt_[:, 0:512], pa_[:, 0:512], AF.Exp, r=[pa_], w=[pt_])
                    act(pt_[:, 512:512 + nb_], pb2[:, 0:nb_], AF.Exp, r=[pb2, pt_], w=[pt_])
                if pend is not None:
                    n_av(pend)
                pend = (h, j, ktl, pt_, nv_, ng_, go, j == 17)
                tick()
        n_av(pend)
        flush()
        flush()
        P.end()

        if debug == "s6":
            break
        P.begin()
        wbr = P.tile("wbr", [128, 16, D], BF16)
        wo_ = P.tile("wo", [128, 8, D], BF16)
        dma("pool", wbr[:], w_branch[l].rearrange("i (k p) d -> p (i k) d", p=128), w=[wbr])
        dma("pool", wo_[:], w_out[l].rearrange("(k p) d -> p k d", p=128), w=[wo_])
        Gc = rot("Gc", [128, 16, 512], BF16, 2)
        mgt = rot("mgt", [128, 4, 512], BF16, 3)
        mrg = rot("mrg", [128, 8, 512], BF16, 2)
        macc = rot("macc", [128, 512], F32, 2)
        mtmp = rot("mtmp", [128, 512], F32, 3)
        pm = rot("pm", [128, 512], F32, 4, psum=True)
        po2 = rot("po2", [128, 512], F32, 4, psum=True)
        xr = rot("xr", [128, D], F32, 2)
        y1 = rot("y1", [128, D], F32, 2)
        y2 = rot("y2", [128, D], F32, 2)
        y3 = rot("y3", [128, D], F32, 2)
        st7 = rot("st7", [128, 16], F32, 2)
        mgr = mg_scr.rearrange("(i d c) q -> c i d q", i=4, c=128)
        for qc in range(5):
            isctx = qc == 4
            n = 256 if isctx else 512
            q0 = 2048 if isctx else qc * 512
            r_ = 1 if isctx else 0
            gc = Gc.next()
            for i in range(4):
                dma("sp" if i % 2 == 0 else "act", gc[:, i * 4:(i + 1) * 4, 0:n],
                    G_scr[i].rearrange("(k c) q -> c k q", c=128)[:, :, q0:q0 + n], r=[gc], w=[gc])
            mr = mrg.next()
            for dch in range(8):
                mg_ = mgt.next()
                dma("sp", mg_[:, :, 0:n], mgr[:, :, dch, q0:q0 + n], w=[mg_])
                ma = macc.next()
                for i in range(4):
                    ps = pm.next()
                    for kc in range(4):
                        mm(ps[:, 0:n], wbr[:, i * 4 + kc, dch * 128:(dch + 1) * 128], gc[:, i * 4 + kc, 0:n], kc == 0, kc == 3,
                           r=[wbr, gc], w=[ps])
                    if i == 0:
                        tt("dve", ma[:, 0:n], ps[:, 0:n], mg_[:, i, 0:n], ALU.mult, r=[ps, mg_], w=[ma])
                    else:
                        tmp = mtmp.next()
                        tt("dve", tmp[:, 0:n], ps[:, 0:n], mg_[:, i, 0:n], ALU.mult, r=[ps, mg_], w=[tmp])
                        if i < 3:
                            tt("pool", ma[:, 0:n], ma[:, 0:n], tmp[:, 0:n], ALU.add, r=[ma, tmp], w=[ma])
                        else:
                            tt("pool", mr[:, dch, 0:n], ma[:, 0:n], tmp[:, 0:n], ALU.add, r=[ma, tmp], w=[mr])
            for ti in range(n // 128):
                tok0 = q0 + ti * 128
                xt_ = xr.next()
                if l == 0:
                    src = hc[ti * 128:(ti + 1) * 128, :] if isctx else hx[tok0:tok0 + 128, :]
                else:
                    src = hctx[ti * 128:(ti + 1) * 128, :] if isctx else hown[tok0:tok0 + 128, :]
                dma("act", xt_[:], src, w=[xt_])
                ya = y1.next()
                for hh in range(2):
                    ps = po2.next()
                    for kc in range(8):
                        mm(ps[:], mr[:, kc, ti * 128:(ti + 1) * 128], wo_[:, kc, hh * 512:(hh + 1) * 512], kc == 0, kc == 7,
                           r=[mr, wo_], w=[ps])
                    tt("dve", ya[:, hh * 512:(hh + 1) * 512], ps[:], grep[r_][:, hh * 512:(hh + 1) * 512], ALU.mult,
                       r=[ps, grep[r_]], w=[ya])
                yb = y2.next()
                stt("dve", yb[:], xt_[:], ALPHA, ya[:], ALU.mult, ALU.add, r=[xt_, ya], w=[yb])
                s_ = st7.next()
                for hh in range(2):
                    P.op("dve", lambda e, s_=s_, yb=yb, hh=hh: e.bn_stats(out=s_[:, hh * 6:(hh + 1) * 6], in_=yb[:, hh * 512:(hh + 1) * 512]),
                         bl([yb, s_]), bl([s_]))
                P.op("dve", lambda e, s_=s_: e.bn_aggr(out=s_[:, 12:14], in_=s_[:, 0:12]), bl([s_]), bl([s_]))
                ts("dve", s_[:, 14:15], s_[:, 13:14], 1e-6, None, ALU.add, r=[s_], w=[s_])
                act(s_[:, 14:15], s_[:, 14:15], AF.Sqrt, r=[s_], w=[s_])
                P.op("dve", lambda e, s_=s_: e.reciprocal(out=s_[:, 14:15], in_=s_[:, 14:15]), bl([s_]), bl([s_]))
                stt("dve", s_[:, 15:16], s_[:, 12:13], -1.0, s_[:, 14:15], ALU.mult, ALU.mult, r=[s_], w=[s_])
                yc = y3.next()
                act(yc[:], yb[:], AF.Identity, bias=s_[:, 15:16], scale=s_[:, 14:15], r=[s_, yb], w=[yc])
                tt("pool", yc[:], yc[:], lng[:], ALU.mult, r=[yc, lng], w=[yc])
                tt("pool", yc[:], yc[:], lnb[:], ALU.add, r=[yc, lnb], w=[yc])
                if l == L - 1:
                    dma("sp", hout[tok0:tok0 + 128, :], yc[:], r=[yc])
                elif isctx:
                    dma("sp", hctx[ti * 128:(ti + 1) * 128, :], yc[:], r=[yc])
                else:
                    dma("sp", hown[tok0:tok0 + 128, :], yc[:], r=[yc])
                    ym0 = ya
                    ym1 = yb
                    act(ym0[:], yc[:], AF.Identity, scale=pmk[:, 1:2], r=[yc, pmk], w=[ym0])
                    ts("dve", ym1[:], yc[:], pmk[:, 0:1], None, ALU.mult, r=[yc, pmk], w=[ym1])
                    dma("sp", agin[tok0:tok0 + 128, :], ym0[:], r=[ym0, aginb[tok0 // 1024]])
                    dma("act", agin[2048 + tok0:2048 + tok0 + 128, :], ym1[:], r=[ym1, aginb[2 + tok0 // 1024]])
            if l < L - 1 and qc in (1, 3):
                for c4 in ((0, 2) if qc == 1 else (1, 3)):
                    P.dma("cc", None, None, [], [aginb[c4], agoutb[c4]], carry=True,
                          fn=lambda e, c4=c4: e.collective_compute("AllReduce", ALU.add, replica_groups=PAIRS,
                                                                  ins=[agin[c4 * 1024:(c4 + 1) * 1024, :].opt()],
                                                                  outs=[agout[c4 * 1024:(c4 + 1) * 1024, :].opt()]))
        P.end()

    P.finish()
    return nc


_TAB = {}


def _tables(half):
    if half in _TAB:
        return _TAB[half]
    own0 = half * 2048
    oth0 = (1 - half) * 2048
    posF = np.concatenate([np.arange(own0, own0 + 2048), np.arange(oth0, oth0 + 2048)])
    kk = np.arange(own0, own0 + 2048)
    m = (posF[:, None].astype(np.int64) * kk[None, :].astype(np.int64)) % 4096
    ang = m.astype(np.float64) * (2.0 * np.pi / 4096.0)
    dftc = (np.cos(ang) / 64.0).astype(ml_dtypes.bfloat16)
    dfts = (np.sin(ang) / 64.0).astype(ml_dtypes.bfloat16)
    n2 = np.arange(256)
    a2 = ((n2[:, None] * n2[None, :]) % 256).astype(np.float64) * (2.0 * np.pi / 256.0)
    dc256 = (np.cos(a2) / 16.0).astype(ml_dtypes.bfloat16)
    ds256 = (np.sin(a2) / 16.0).astype(ml_dtypes.bfloat16)
    c2 = np.arange(128)
    a3 = ((c2[:, None] * c2[None, :]) % 128).astype(np.float64) * (2.0 * np.pi / 128.0)
    ccc = (np.cos(a3) / np.sqrt(128.0)).astype(ml_dtypes.bfloat16)
    nscc = (-np.sin(a3) / np.sqrt(128.0)).astype(ml_dtypes.bfloat16)
    inv = (np.float32(10000.0) ** (-np.arange(16, dtype=np.float32) / np.float32(16))).astype(np.float32)
    rows = (posF // 64).astype(np.float32)
    cols = (posF % 64).astype(np.float32)
    angs = np.stack([rows[:, None] * inv[None, :], cols[:, None] * inv[None, :]], axis=0).astype(np.float32)
    cosv = np.cos(angs).astype(np.float32)
    sinv = np.sin(angs).astype(np.float32)
    ropec = np.zeros((128, 4096), np.float32)
    ropes = np.zeros((128, 4096), np.float32)
    for comp in range(2):
        for ax in range(2):
            for s in range(2):
                p0 = comp * 64 + ax * 32 + s * 16
                ropec[p0:p0 + 16, :] = cosv[ax].T
                ropes[p0:p0 + 16, :] = (-sinv[ax].T if s == 0 else sinv[ax].T)
    pinv = np.zeros((4, NQ), np.float32)
    for g, w in enumerate((2, 4, 8, 16)):
        t = np.arange(own0, own0 + 2048)
        lo = np.clip(t - w // 2, 0, 4096)
        hi = np.clip(t - w // 2 + w, 0, 4096)
        pinv[g, :2048] = (1.0 / (hi - lo).astype(np.float64)).astype(np.float32)
        t = np.arange(256)
        lo = np.clip(t - w // 2, 0, 256)
        hi = np.clip(t - w // 2 + w, 0, 256)
        pinv[g, 2048:] = (1.0 / (hi - lo).astype(np.float64)).astype(np.float32)
    pmask = np.array([1.0, 0.0] if half == 1 else [0.0, 1.0], np.float32)
    ri = np.zeros((5, 6, 128, 128), np.int64)
    ci = np.zeros((5, 6, 128, 128), np.int64)
    va = np.zeros((5, 6, 128, 128), bool)
    krl = (np.arange(128) // 64)[:, None]
    kc = (np.arange(128) % 64)[:, None]
    qrl = (np.arange(128) // 64)[None, :]
    qc = (np.arange(128) % 64)[None, :]
    for pat, j in enumerate((4, 0, 1, 14, 15)):
        r = 32 * half + 2 * j
        qrow = r + qrl
        rs = np.clip(qrow - 4, 0, 56)
        cs = np.clip(qc - 8, 0, 48)
        alist = list(range(6)) if j == 0 else (list(range(-1, 5)) if j == 15 else list(range(5)))
        for slot, a in enumerate(alist):
            gt = 16 * half + j - 2 + a
            kr = 2 * gt + krl
            ok = (gt >= 0) & (gt <= 31) & (kr >= rs) & (kr <= rs + 7) & (kc >= cs) & (kc <= cs + 15)
            ok = np.broadcast_to(ok, (128, 128))
            ri[pat, slot] = np.clip(np.broadcast_to(kr - qrow + 7, (128, 128)), 0, 14)
            ci[pat, slot] = np.clip(np.broadcast_to(kc - qc + 15, (128, 128)), 0, 30)
            va[pat, slot] = ok
    T = dict(dftc=dftc, dfts=dfts, dc256=dc256, ds256=ds256, ccc=ccc, nscc=nscc, ropec=ropec, ropes=ropes,
             pinv=pinv, pmask=pmask, identf=np.eye(128, dtype=np.float32), ri=ri, ci=ci, va=va)
    _TAB[half] = T
    return T


def _nab_table(na_bias_l, T):
    g = na_bias_l[:, T["ri"], T["ci"]]
    g = np.where(T["va"][None], g, np.float32(-30000.0)).astype(np.float32)
    return np.ascontiguousarray(g.transpose(0, 3, 1, 2, 4)).reshape(8, 128, 3840)


_NC = {}


def _get_nc(n_layers, debug=False):
    key = (n_layers, debug)
    if key not in _NC:
        _NC[key] = build_program(n_layers, debug)
    return _NC[key]


def _core_inputs(core, h_lat, h_ctx, c, c_ctx, W, layers):
    b, half = core // 2, core % 2
    T = _tables(half)
    own = h_lat[b, half * 2048:(half + 1) * 2048]
    oth = h_lat[b, (1 - half) * 2048:(2 - half) * 2048]
    m = {
        "hx": np.ascontiguousarray(np.concatenate([own, oth], axis=0)),
        "hc": np.ascontiguousarray(h_ctx[b]),
        "cvec": np.ascontiguousarray(np.stack([c[b], c_ctx], axis=0).reshape(16, 128)),
    }
    sl = slice(layers[0], layers[-1] + 1)
    for k in ("w_mod", "b_mod", "w_in", "b_in", "fnet_w", "diff_subln", "pool_w", "pool_scale", "w_branch", "w_out", "ln_g", "ln_b"):
        m[k] = np.ascontiguousarray(W[k][sl])
    m["diff_lam"] = np.ascontiguousarray(W["diff_lam"][sl].reshape(len(layers), 256))
    m["nab"] = np.stack([_nab_table(W["na_bias"][l], T) for l in layers], axis=0)
    m["lamc"] = np.array([[0.8 - 0.6 * math.exp(-0.3 * l), 1.0 - (0.8 - 0.6 * math.exp(-0.3 * l))] for l in layers], np.float32)
    for k in ("dftc", "dfts", "dc256", "ds256", "ccc", "nscc", "ropec", "ropes", "pinv", "pmask", "identf"):
        m[k] = T[k]
    return m


def kernel(x, c, ctx, c_ctx, w_mod, b_mod, w_in, b_in, fnet_w, diff_lam, diff_subln, pool_w, pool_scale, na_bias,
           w_branch, w_out, ln_g, ln_b):
    W = dict(w_mod=w_mod, b_mod=b_mod, w_in=w_in, b_in=b_in, fnet_w=fnet_w, diff_lam=diff_lam, diff_subln=diff_subln,
             pool_w=pool_w, pool_scale=pool_scale, na_bias=na_bias, w_branch=w_branch, w_out=w_out, ln_g=ln_g, ln_b=ln_b)
    W = {k: np.asarray(v, np.float32) for k, v in W.items()}
    h_lat = np.asarray(x, np.float32)
    h_ctx = np.asarray(ctx, np.float32)
    c = np.asarray(c, np.float32)
    c_ctx = np.asarray(c_ctx, np.float32)
    nc = _get_nc(4)
    in_maps = [_core_inputs(core, h_lat, h_ctx, c, c_ctx, W, [0, 1, 2, 3]) for core in range(8)]
    res = run_bass_kernel_spmd(nc, in_maps, core_ids=list(range(8)))
    out = np.empty_like(h_lat)
    for core in range(8):
        b, half = core // 2, core % 2
        out[b, half * 2048:(half + 1) * 2048] = np.asarray(res.results[core]["hout"])[0:2048]
    return out
```

```python
import contextlib
import numpy as np
import concourse.bass as bass
import concourse.mybir as mybir

F32 = mybir.dt.float32
BF16 = mybir.dt.bfloat16
AF = mybir.ActivationFunctionType
ALU = mybir.AluOpType

ENGS = ("pe", "act", "dve", "pool", "sp")
HANDLES = {"pe": "tensor", "act": "scalar", "dve": "vector", "pool": "gpsimd", "sp": "sync"}


class Buf:
    __slots__ = ("name", "writer", "readers")

    def __init__(self, name=""):
        self.name = name
        self.writer = None
        self.readers = []


class Tile:
    def __init__(self, t, name):
        self.t = t
        self.b = Buf(name)

    def __getitem__(self, k):
        return self.t[k]


class Op:
    __slots__ = ("eng", "fn", "deps", "ticket", "signal", "is_dma", "slot", "dcount", "prev", "phase", "q", "carry")

    def __init__(self, eng, fn, phase, is_dma=False):
        self.eng = eng
        self.fn = fn
        self.deps = []
        self.ticket = None
        self.signal = False
        self.is_dma = is_dma
        self.slot = None
        self.dcount = None
        self.prev = None
        self.phase = phase
        self.carry = False


class Prog:
    def __init__(self, nc, nslots=None):
        self.nc = nc
        self.nslots = nslots or {"sp": 16, "pool": 16, "act": 8, "cc": 1}
        self.qinc = {"sp": 16, "pool": 16, "act": 16, "cc": 1}
        self.qeng = {"sp": "sp", "pool": "pool", "act": "act", "cc": "pool"}
        self.gstack = contextlib.ExitStack()
        self.esem = {e: self.gstack.enter_context(nc.semaphore("s_" + e)) for e in ENGS}
        self.dsem = {q: [self.gstack.enter_context(nc.semaphore("d_%s%d" % (q, i))) for i in range(n)]
                     for q, n in self.nslots.items()}
        self.slot_rr = {q: 0 for q in self.nslots}
        self.slot_last = {q: [None] * n for q, n in self.nslots.items()}
        self.slot_count = {q: [0] * n for q, n in self.nslots.items()}
        self.ecount = {e: 0 for e in ENGS}
        self.waited = {e: {} for e in ENGS}
        self.phase = 0
        self.pstack = None
        self.ops = {e: [] for e in ENGS}
        self.uid = 0
        self.nblocks = 0
        self.clear_all()

    def clear_all(self):
        sems = list(self.esem.values()) + [s for q in self.dsem.values() for s in q]
        with self.nc.Block() as block:
            def body(eng):
                for s in sems:
                    eng.sem_clear(s)
            block.gpsimd(body)

    def _nm(self, name):
        self.uid += 1
        return "%s_%d" % (name, self.uid)

    def gtile(self, name, shape, dtype=F32):
        return Tile(self.gstack.enter_context(self.nc.sbuf_tensor(self._nm(name), list(shape), dtype)), name)

    def tile(self, name, shape, dtype=F32):
        return Tile(self.pstack.enter_context(self.nc.sbuf_tensor(self._nm(name), list(shape), dtype)), name)

    def ptile(self, name, shape, dtype=F32):
        return Tile(self.pstack.enter_context(self.nc.psum_tensor(self._nm(name), list(shape), dtype)), name)

    def begin(self):
        assert self.pstack is None
        self.pstack = contextlib.ExitStack()
        self.phase += 1
        self.ops = {e: [] for e in ENGS}

    def end(self):
        self._emit()
        self.pstack.close()
        self.pstack = None

    def finish(self):
        self.clear_all()
        self.gstack.close()

    def _add_deps(self, op, reads, writes):
        deps = []
        for b in reads:
            if b.writer is not None:
                deps.append(b.writer)
        for b in writes:
            if b.writer is not None:
                deps.append(b.writer)
            deps.extend(b.readers)
        seen = set()
        for d in deps:
            if d is op or id(d) in seen or (d.phase != self.phase and not d.carry):
                continue
            seen.add(id(d))
            if d.eng == "pe" and op.eng == "pe" and not d.is_dma and not op.is_dma:
                continue
            op.deps.append(d)
            d.signal = True
        for b in reads:
            b.readers.append(op)
        for b in writes:
            b.writer = op
            b.readers = []

    def op(self, eng, fn, reads=(), writes=()):
        o = Op(eng, fn, self.phase)
        self._add_deps(o, reads, writes)
        self.ops[eng].append(o)
        return o

    def dma(self, q, out, in_, reads=(), writes=(), fn=None, carry=False, **kw):
        o = Op(self.qeng[q], None, self.phase, is_dma=True)
        o.q = q
        o.carry = carry
        o.fn = fn if fn is not None else (lambda e: e.dma_start(out=out, in_=in_, **kw))
        n = self.nslots[q]
        s = self.slot_rr[q]
        self.slot_rr[q] = (s + 1) % n
        o.slot = s
        o.prev = self.slot_last[q][s]
        self.slot_count[q][s] += self.qinc[q]
        o.dcount = self.slot_count[q][s]
        self.slot_last[q][s] = o
        o.signal = True
        self._add_deps(o, reads, writes)
        self.ops[self.qeng[q]].append(o)
        return o

    def _emit(self):
        nc = self.nc
        for e in ENGS:
            c = self.ecount[e]
            for o in self.ops[e]:
                if o.is_dma:
                    continue
                if o.signal:
                    c += 1
                    o.ticket = c
            self.ecount[e] = c
        if not any(self.ops[e] for e in ENGS):
            return
        self.nblocks += 1
        with nc.Block() as block:
            for e in ENGS:
                ops = self.ops[e]
                if not ops:
                    continue

                def body(eng, ops=ops, e=e):
                    waited = self.waited[e]

                    def wait(key, sem, val):
                        if waited.get(key, 0) >= val:
                            return
                        waited[key] = val
                        eng.wait_ge(sem, val)

                    for o in ops:
                        for d in o.deps:
                            if d.is_dma:
                                wait(("d", d.q, d.slot), self.dsem[d.q][d.slot], d.dcount)
                            else:
                                wait(("e", d.eng), self.esem[d.eng], d.ticket)
                        if o.is_dma:
                            p = o.prev
                            if p is not None and (p.phase == o.phase or p.carry):
                                wait(("d", p.q, p.slot), self.dsem[p.q][p.slot], p.dcount)
                            if o.q == "cc":
                                o.fn(eng).then_inc(self.dsem[o.q][o.slot])
                            else:
                                o.fn(eng).then_inc(self.dsem[o.q][o.slot], 16)
                        else:
                            ins = o.fn(eng)
                            if o.signal:
                                ins.then_inc(self.esem[e], 1)
                    for q in self.nslots:
                        if self.qeng[q] != e:
                            continue
                        for s, last in enumerate(self.slot_last[q]):
                            if last is not None and last.phase == self.phase and not last.carry:
                                wait(("d", q, s), self.dsem[q][s], last.dcount)

                getattr(block, HANDLES[e])(body)
from concourse.bass_utils import run_bass_kernel_spmd

import math
import ml_dtypes

D = 1024
NF = 4352
NQ = 2304
NKN = 2816
ALPHA = (2.0 * 4) ** 0.25
AX = mybir.AxisListType.X


def build_program(n_layers=1, debug=False):
    nc = bass.Bass("TRN2", target_bir_lowering=False)
    L = n_layers

    def din(name, shape, dt=F32):
        return nc.dram_tensor(name, list(shape), dt, kind="ExternalInput").ap()

    def dscr(name, shape, dt=BF16):
        return nc.dram_tensor(name, list(shape), dt, kind="ExternalOutput" if debug else "Internal").ap()

    hx = din("hx", [4096, D])
    hc = din("hc", [256, D])
    cvec = din("cvec", [16, 128])
    w_mod = din("w_mod", [L, D, 3 * D])
    b_mod = din("b_mod", [L, 3 * D])
    w_in = din("w_in", [L, D, 10240])
    b_in = din("b_in", [L, 10240])
    fnet_w = din("fnet_w", [L, 4, 128, 128])
    diff_lam = din("diff_lam", [L, 256])
    diff_subln = din("diff_subln", [L, 128])
    pool_w = din("pool_w", [L, 4, 128, 128])
    pool_scale = din("pool_scale", [L, 512])
    w_branch = din("w_branch", [L, 4, 512, D])
    w_out = din("w_out", [L, D, D])
    ln_g = din("ln_g", [L, D])
    ln_b = din("ln_b", [L, D])
    nab = din("nab", [L, 8, 128, 3840])
    lamc = din("lamc", [L, 2])
    dftc = din("dftc", [4096, 2048], BF16)
    dfts = din("dfts", [4096, 2048], BF16)
    dc256 = din("dc256", [256, 256], BF16)
    ds256 = din("ds256", [256, 256], BF16)
    ccc = din("ccc", [128, 128], BF16)
    nscc = din("nscc", [128, 128], BF16)
    ropec = din("ropec", [128, 4096])
    ropes = din("ropes", [128, 4096])
    pinv = din("pinv", [4, NQ])
    pmask = din("pmask", [2])
    identf = din("identf", [128, 128])
    hout = nc.dram_tensor("hout", [NQ, D], F32, kind="ExternalOutput").ap()

    u_scr = dscr("u_scr", [NF, 512])
    gate_scr = [dscr("gate_scr%d" % i, [512, NQ]) for i in range(4)]
    dq_scr = dscr("dq_scr", [512, NQ])
    dk_scr = dscr("dk_scr", [512, NF])
    dv_scr = dscr("dv_scr", [NF, 4, 129])
    pin_scr = dscr("pin_scr", [512, 2320], F32)
    nq_scr = dscr("nq_scr", [512, NQ])
    nk_scr = dscr("nk_scr", [512, NKN])
    nv_scr = dscr("nv_scr", [NKN, 8, 65])
    mg_scr = dscr("mg_scr", [4096, NQ])
    G_scr = [dscr("G_scr%d" % i, [512, NQ]) for i in range(4)]

    hown = nc.dram_tensor("hown", [2048, D], F32).ap()
    hctx = nc.dram_tensor("hctx", [256, D], F32).ap()
    agin = nc.dram_tensor("agin", [4096, D], F32).ap()
    agout = nc.dram_tensor("agout", [4096, D], F32).ap()
    PAIRS = [[0, 1], [2, 3], [4, 5], [6, 7]]
    aginb = [Buf("agin%d" % i) for i in range(4)]
    agoutb = [Buf("agout%d" % i) for i in range(4)]

    P = Prog(nc)

    def bl(x):
        return [t.b if hasattr(t, "b") else t for t in x]

    def act(out, in_, func, bias=None, scale=None, accum_out=None, r=(), w=()):
        kw = {}
        if bias is not None:
            kw["bias"] = bias
        if scale is not None:
            kw["scale"] = scale
        if accum_out is not None:
            kw["accum_out"] = accum_out
        P.op("act", lambda e: e.activation(out=out, in_=in_, func=func, **kw), bl(r), bl(w))

    def ts(eng, out, in0, s1, s2, op0, op1=None, r=(), w=()):
        if op1 is None:
            P.op(eng, lambda e: e.tensor_scalar(out=out, in0=in0, scalar1=s1, scalar2=None, op0=op0), bl(r), bl(w))
        else:
            P.op(eng, lambda e: e.tensor_scalar(out=out, in0=in0, scalar1=s1, scalar2=s2, op0=op0, op1=op1), bl(r), bl(w))

    def tt(eng, out, in0, in1, op, r=(), w=()):
        P.op(eng, lambda e: e.tensor_tensor(out=out, in0=in0, in1=in1, op=op), bl(r), bl(w))

    def stt(eng, out, in0, scalar, in1, op0, op1, r=(), w=()):
        P.op(eng, lambda e: e.scalar_tensor_tensor(out=out, in0=in0, scalar=scalar, in1=in1, op0=op0, op1=op1), bl(r), bl(w))

    def cp(eng, out, in_, r=(), w=()):
        P.op(eng, lambda e: e.tensor_copy(out=out, in_=in_), bl(r), bl(w))

    def mm(out, lhsT, rhs, start, stop, r=(), w=(), skip=False):
        P.op("pe", lambda e: e.matmul(out, lhsT, rhs, start=start, stop=stop, skip_group_check=skip), bl(r), bl(w))

    def tr(out, in_, ident, r=(), w=()):
        P.op("pe", lambda e: e.transpose(out, in_, ident), bl(r), bl(w))

    def dma(q, out, in_, r=(), w=(), **kw):
        P.dma(q, out, in_, bl(r), bl(w), **kw)

    def memset(eng, ap, val, w=()):
        P.op(eng, lambda e: e.memset(ap, val), (), bl(w))

    class Rot:
        def __init__(self, tiles):
            self.tiles = tiles
            self.i = 0

        def next(self):
            t = self.tiles[self.i % len(self.tiles)]
            self.i += 1
            return t

    class View:
        def __init__(self, ap, name):
            self.ap = ap
            self.b = Buf(name)

        def __getitem__(self, k):
            return self.ap[k]

    def bank_views(name, n, width, dtype):
        t = P.ptile(name, [128, n * width], dtype)
        return Rot([View(t[:, i * width:(i + 1) * width], "%s%d" % (name, i)) for i in range(n)])

    def rot(name, shape, dtype, n, psum=False):
        mk = P.ptile if psum else P.tile
        return Rot([mk("%s%d" % (name, i), shape, dtype) for i in range(n)])

    identF = P.gtile("identF", [128, 128], F32)
    identB = P.gtile("identB", [128, 128], BF16)
    ones = P.gtile("ones", [128, 128], F32)
    bfm = P.gtile("bfm", [128, 116], F32)
    modfm = P.gtile("modfm", [128, 24, 2], F32)
    opsc = P.gtile("opsc", [128, 8, 2], F32)
    grep = [P.gtile("grep%d" % r, [128, D], F32) for r in range(2)]
    lng = P.gtile("lng", [128, D], F32)
    lnb = P.gtile("lnb", [128, D], F32)
    bu = P.gtile("bu", [128, 512], F32)
    bdv = P.gtile("bdv", [128, 512], F32)
    bnv = P.gtile("bnv", [128, 512], F32)
    subl = P.gtile("subl", [128, 128], F32)
    lamt = P.gtile("lamt", [128, 8], F32)
    pmk = P.gtile("pmk", [128, 2], F32)
    cccT = P.gtile("cccT", [128, 128], BF16)
    nsccT = P.gtile("nsccT", [128, 128], BF16)
    fwT = P.gtile("fwT", [128, 4, 128], BF16)
    pwT = P.gtile("pwT", [128, 4, 128], BF16)

    for l in range(L):
        P.begin()
        if l == 0:
            dma("sp", identF[:], identf, w=[identF])
            dma("pool", identB[:], identf, w=[identB])
            memset("dve", ones[:], 1.0, w=[ones])
            dma("sp", cccT[:], ccc, w=[cccT])
            dma("sp", nsccT[:], nscc, w=[nsccT])
            dma("sp", pmk[:], pmask.partition_broadcast(128), w=[pmk])
        dma("pool", fwT[:], fnet_w[l].rearrange("g c d -> c g d"), w=[fwT])
        dma("pool", pwT[:], pool_w[l].rearrange("g c d -> c g d"), w=[pwT])
        sr = P.tile("sr", [128, 128], F32)
        memset("dve", sr[:], 0.0, w=[sr])
        dma("sp", sr[0:80, :], b_in[l].rearrange("(j p) -> j p", p=128), r=[sr], w=[sr])
        bsrc = b_in[l, 1024:2048].rearrange("(j g s f) -> j g s f", g=4, s=2, f=16)
        bdst = sr[80:88, :].rearrange("j (g s f) -> j g s f", g=4, s=2, f=16)
        dma("sp", bdst[:, :, 0, :], bsrc[:, :, 1, :], r=[sr], w=[sr])
        dma("sp", bdst[:, :, 1, :], bsrc[:, :, 0, :], r=[sr], w=[sr])
        dma("sp", sr[88:112, :], b_mod[l].rearrange("(j p) -> j p", p=128), r=[sr], w=[sr])
        dma("sp", sr[112:116, :], pool_scale[l].rearrange("(j p) -> j p", p=128), r=[sr], w=[sr])
        sr2 = P.tile("sr2", [16, 128], F32)
        dma("sp", sr2[:], cvec, w=[sr2])
        pz = P.ptile("pz", [128, 512], F32)
        tr(pz[:, 0:128], sr[:], identF[:], r=[sr, identF], w=[pz])
        cp("dve", bfm[:], pz[:, 0:116], r=[pz], w=[bfm])
        tr(pz[:, 128:144], sr2[:], identF[0:16, 0:16], r=[sr2, identF], w=[pz])
        csil = P.tile("csil", [128, 16], F32)
        act(csil[:], pz[:, 128:144], AF.Silu, r=[pz], w=[csil])
        crep = [P.tile("crep%d" % r, [128, 8, 128], F32) for r in range(2)]
        for r_ in range(2):
            for kc in range(8):
                ts("dve", crep[r_][:, kc, :], ones[:], csil[:, r_ * 8 + kc:r_ * 8 + kc + 1], None, ALU.mult,
                   r=[ones, csil], w=[crep[r_]])
        brep = P.tile("brep", [128, D], F32)
        dma("sp", brep[:], b_mod[l, 2048:3072].partition_broadcast(128), w=[brep])
        dma("sp", lng[:], ln_g[l].partition_broadcast(128), w=[lng])
        dma("sp", lnb[:], ln_b[l].partition_broadcast(128), w=[lnb])
        dma("sp", bu[:], b_in[l, 0:512].partition_broadcast(128), w=[bu])
        dma("sp", bdv[:], b_in[l, 2048:2560].partition_broadcast(128), w=[bdv])
        dma("sp", bnv[:], b_in[l, 5120:5632].partition_broadcast(128), w=[bnv])
        dma("sp", subl[:], diff_subln[l].partition_broadcast(128), w=[subl])
        dlam = P.tile("dlam", [128, 256], F32)
        dma("sp", dlam[:], diff_lam[l].partition_broadcast(128), w=[dlam])
        dma("sp", lamt[:, 0:2], lamc[l].partition_broadcast(128), w=[lamt])
        ltmp = P.tile("ltmp", [128, 128], F32)
        tt("dve", ltmp[:, 0:64], dlam[:, 0:64], dlam[:, 64:128], ALU.mult, r=[dlam], w=[ltmp])
        tt("dve", ltmp[:, 64:128], dlam[:, 128:192], dlam[:, 192:256], ALU.mult, r=[dlam, ltmp], w=[ltmp])
        P.op("dve", lambda e: e.tensor_reduce(out=lamt[:, 2:4], in_=ltmp[:].rearrange("p (a b) -> p a b", a=2), axis=AX, op=ALU.add),
             bl([ltmp, lamt]), bl([lamt]))
        act(lamt[:, 2:4], lamt[:, 2:4], AF.Exp, r=[lamt], w=[lamt])
        tt("dve", lamt[:, 4:5], lamt[:, 2:3], lamt[:, 3:4], ALU.subtract, r=[lamt], w=[lamt])
        tt("dve", lamt[:, 4:5], lamt[:, 4:5], lamt[:, 0:1], ALU.add, r=[lamt], w=[lamt])
        ts("dve", lamt[:, 5:6], lamt[:, 4:5], -1.0, None, ALU.mult, r=[lamt], w=[lamt])
        ts("dve", subl[:], subl[:], lamt[:, 1:2], None, ALU.mult, r=[subl, lamt], w=[subl])
        wm = rot("wm", [128, 8, 512], F32, 2)
        pg = [P.ptile("pg%d" % i, [128, 512], F32) for i in range(4)]
        wmr = w_mod[l].rearrange("(k p) n -> p k n", p=128)
        for pc in range(6):
            wt = wm.next()
            dma("sp" if pc % 2 == 0 else "act", wt[:], wmr[:, :, pc * 512:(pc + 1) * 512], w=[wt])
            for jj in range(4):
                j = pc * 4 + jj
                for kc in range(8):
                    mm(pz[:, 256 + 2 * j:256 + 2 * j + 2], wt[:, kc, jj * 128:(jj + 1) * 128],
                       csil[:].rearrange("p (r k) -> p k r", r=2)[:, kc, :], kc == 0, kc == 7,
                       r=[wt, csil], w=[pz])
            if pc >= 4:
                for r_ in range(2):
                    pgt = pg[(pc - 4) * 2 + r_]
                    for kc in range(8):
                        mm(pgt[:], crep[r_][:, kc, :], wt[:, kc, :], kc == 0, kc == 7, r=[crep[r_], wt], w=[pgt])
                    hh = pc - 4
                    tt("dve", grep[r_][:, hh * 512:(hh + 1) * 512], pgt[:], brep[:, hh * 512:(hh + 1) * 512], ALU.add,
                       r=[pgt, brep], w=[grep[r_]])
        for r_ in range(2):
            tt("dve", modfm[:, :, r_], pz[:, 256:304].rearrange("p (j r) -> p j r", r=2)[:, :, r_], bfm[:, 88:112], ALU.add,
               r=[pz, bfm], w=[modfm])
        ts("dve", opsc[:], modfm[:, 8:16, :], 1.0, None, ALU.add, r=[modfm], w=[opsc])
        P.end()

        P.begin()
        xT = P.tile("xT", [128, 8, NF], BF16)
        xTb = [Buf("xT%d" % i) for i in range(34)]

        def xtr(s0, n):
            return [xTb[t] for t in range(s0 // 128, (s0 + n - 1) // 128 + 1)]
        xin = rot("xin", [128, D], F32, 2)
        xin2 = rot("xin2", [128, D], F32, 2)
        yln = rot("yln", [128, D], F32, 2)
        stt_ = rot("st", [128, 16], F32, 2)
        ptr = rot("ptr", [128, 512], F32, 2, psum=True)
        def ln_tile(tI):
            r_ = 0 if tI < 32 else 1
            xt_ = xin.next()
            yt_ = yln.next()
            s_ = stt_.next()
            if l == 0:
                src = hx[tI * 128:(tI + 1) * 128, :] if tI < 32 else hc[(tI - 32) * 128:(tI - 31) * 128, :]
                dma("sp", xt_[:], src, w=[xt_])
            elif tI < 16:
                dma("sp", xt_[:], hown[tI * 128:(tI + 1) * 128, :], w=[xt_])
            elif tI >= 32:
                dma("sp", xt_[:], hctx[(tI - 32) * 128:(tI - 31) * 128, :], w=[xt_])
            else:
                jj = tI - 16
                xb_ = xin2.next()
                dma("sp", xt_[:], agout[jj * 128:(jj + 1) * 128, :], r=[agoutb[jj // 8]], w=[xt_])
                dma("act", xb_[:], agout[2048 + jj * 128:2048 + (jj + 1) * 128, :], r=[agoutb[2 + jj // 8]], w=[xb_])
                act(xt_[:], xt_[:], AF.Identity, scale=pmk[:, 0:1], r=[xt_, pmk], w=[xt_])
                stt("dve", xt_[:], xb_[:], pmk[:, 1:2], xt_[:], ALU.mult, ALU.add, r=[xb_, pmk, xt_], w=[xt_])
            for hh in range(2):
                P.op("dve", lambda e, s_=s_, xt_=xt_, hh=hh: e.bn_stats(out=s_[:, hh * 6:(hh + 1) * 6], in_=xt_[:, hh * 512:(hh + 1) * 512]),
                     bl([xt_, s_]), bl([s_]))
            P.op("dve", lambda e, s_=s_: e.bn_aggr(out=s_[:, 12:14], in_=s_[:, 0:12]), bl([s_]), bl([s_]))
            ts("dve", s_[:, 14:15], s_[:, 13:14], 1e-6, None, ALU.add, r=[s_], w=[s_])
            act(s_[:, 14:15], s_[:, 14:15], AF.Sqrt, r=[s_], w=[s_])
            P.op("dve", lambda e, s_=s_: e.reciprocal(out=s_[:, 14:15], in_=s_[:, 14:15]), bl([s_]), bl([s_]))
            stt("dve", s_[:, 15:16], s_[:, 12:13], -1.0, s_[:, 14:15], ALU.mult, ALU.mult, r=[s_], w=[s_])
            act(yt_[:], xt_[:], AF.Identity, bias=s_[:, 15:16], scale=s_[:, 14:15], r=[s_, xt_], w=[yt_])
            for hh in range(2):
                pt = ptr.next()
                for k4 in range(4):
                    kc = hh * 4 + k4
                    tr(pt[:, k4 * 128:(k4 + 1) * 128], yt_[:, kc * 128:(kc + 1) * 128], identF[:], r=[yt_, identF], w=[pt])
                for k4 in range(4):
                    kc = hh * 4 + k4
                    if k4 % 2 == 0:
                        act(xT[:, kc, tI * 128:(tI + 1) * 128], pt[:, k4 * 128:(k4 + 1) * 128], AF.Identity,
                            bias=modfm[:, kc, r_:r_ + 1], scale=opsc[:, kc, r_:r_ + 1], r=[pt, modfm, opsc], w=[xTb[tI]])
                    else:
                        ts("dve", xT[:, kc, tI * 128:(tI + 1) * 128], pt[:, k4 * 128:(k4 + 1) * 128],
                           opsc[:, kc, r_:r_ + 1], modfm[:, kc, r_:r_ + 1], ALU.mult, ALU.add, r=[pt, modfm, opsc], w=[xTb[tI]])

        wbf = rot("wbf", [128, 8, 512], BF16, 2)
        wpm = P.tile("wpm", [128, 8, 512], BF16)
        pp = rot("pp", [128, 512], F32, 4, psum=True)
        ofm = rot("ofm", [128, NF], BF16, 2)
        ofm32 = P.tile("ofm32", [128, 2320], F32)
        otk = rot("otk", [128, 520], BF16, 3)
        rtab = rot("rtab", [128, 2, 512], F32, 2)
        rt1 = rot("rt1", [128, 512], F32, 2)
        rt2 = rot("rt2", [128, 512], F32, 2)
        orp = rot("orp", [128, 512], BF16, 3)
        w_r = w_in[l].rearrange("(k p) n -> p k n", p=128)
        evac_i = [0]

        def load_w(grp):
            wt = wbf.next()
            dma("pool", wt[:], w_r[:, :, grp * 512:(grp + 1) * 512], w=[wt])
            return wt

        def chunks(ranges):
            out = []
            for (s0, n, d0) in ranges:
                o = 0
                while o < n:
                    c = min(512, n - o)
                    out.append((s0 + o, c, d0 + o))
                    o += c
            return out

        QR = [(0, 2048, 0), (4096, 256, 2048)]
        NKR = [(3840, 256, 0), (0, 2304, 256), (4096, 256, 2560)]
        PLR = [(4088, 8, 0), (0, 2056, 8), (4096, 256, 2064)]

        def fm_group(grp, ranges, dst, mode, ncols):
            wt = load_w(grp)
            for g4 in range(4):
                gcol = grp * 4 + g4
                ot = ofm32 if mode == "pool" else ofm.next()
                for (s0, n, d0) in chunks(ranges):
                    ps = pp.next()
                    for kc in range(8):
                        mm(ps[:, 0:n], wt[:, kc, g4 * 128:(g4 + 1) * 128], xT[:, kc, s0:s0 + n], kc == 0, kc == 7,
                           r=[wt] + xtr(s0, n), w=[ps])
                    bcol = bfm[:, gcol:gcol + 1]
                    if mode == "silu":
                        act(ot[:, d0:d0 + n], ps[:, 0:n], AF.Silu, bias=bcol, r=[ps, bfm], w=[ot])
                    elif mode == "sigmoid":
                        act(ot[:, d0:d0 + n], ps[:, 0:n], AF.Sigmoid, bias=bcol, r=[ps, bfm], w=[ot])
                    elif mode == "q8":
                        ts("dve", ot[:, d0:d0 + n], ps[:, 0:n], bcol, 0.125, ALU.add, ALU.mult, r=[ps, bfm], w=[ot])
                    else:
                        evac_i[0] += 1
                        if evac_i[0] % 2 == 0:
                            act(ot[:, d0:d0 + n], ps[:, 0:n], AF.Identity, bias=bcol, r=[ps, bfm], w=[ot])
                        else:
                            ts("dve", ot[:, d0:d0 + n], ps[:, 0:n], bcol, None, ALU.add, r=[ps, bfm], w=[ot])
                dma("sp", dst[g4 * 128:(g4 + 1) * 128, :], ot[:, 0:ncols], r=[ot])

        def tm_group(grp, tok_tiles, dst_fn, brep_t, nh, hd):
            wt = load_w(grp)
            for i, s0 in enumerate(tok_tiles):
                ps = pp.next()
                for kc in range(8):
                    mm(ps[:], xT[:, kc, s0:s0 + 128], wt[:, kc, :], kc == 0, kc == 7, r=[wt] + xtr(s0, 128), w=[ps])
                ot = otk.next()
                if nh == 1:
                    tt("dve", ot[:, 0:512], ps[:], brep_t[:], ALU.add, r=[ps, brep_t], w=[ot])
                    dma("sp", dst_fn(i), ot[:, 0:512], r=[ot])
                else:
                    ov = ot[:, 0:nh * (hd + 1)].rearrange("p (h e) -> p h e", h=nh)
                    tt("dve", ov[:, :, 0:hd], ps[:].rearrange("p (h e) -> p h e", h=nh),
                       brep_t[:].rearrange("p (h e) -> p h e", h=nh), ALU.add, r=[ps, brep_t], w=[ot])
                    memset("pool", ov[:, :, hd:hd + 1], 1.0, w=[ot])
                    dma("sp", dst_fn(i), ov, r=[ot])

        def rope_group(grp, ranges, dst, boff, swoff):
            wt = load_w(grp)
            wv = wt[:].rearrange("p k (g s f) -> p k g s f", s=2, f=16)
            pv = wpm[:].rearrange("p k (g s f) -> p k g s f", s=2, f=16)
            cp("pool", pv[:, :, :, 0, :], wv[:, :, :, 1, :], r=[wt], w=[wpm])
            cp("pool", pv[:, :, :, 1, :], wv[:, :, :, 0, :], r=[wt, wpm], w=[wpm])
            for (s0, n, d0) in chunks(ranges):
                isctx = s0 >= 4096
                if not isctx:
                    rtb = rtab.next()
                    dma("act", rtb[:, 0, 0:n], ropec[:, s0:s0 + n], w=[rtb])
                    dma("act", rtb[:, 1, 0:n], ropes[:, s0:s0 + n], r=[rtb], w=[rtb])
                for g4 in range(4):
                    psa = pp.next()
                    for kc in range(8):
                        mm(psa[:, 0:n], wt[:, kc, g4 * 128:(g4 + 1) * 128], xT[:, kc, s0:s0 + n], kc == 0, kc == 7,
                           r=[wt] + xtr(s0, n), w=[psa])
                    ot = orp.next()
                    bcol = bfm[:, boff + g4:boff + g4 + 1]
                    if isctx:
                        act(ot[:, 0:n], psa[:, 0:n], AF.Identity, bias=bcol, r=[psa, bfm], w=[ot])
                    else:
                        psb = pp.next()
                        for kc in range(8):
                            mm(psb[:, 0:n], wpm[:, kc, g4 * 128:(g4 + 1) * 128], xT[:, kc, s0:s0 + n], kc == 0, kc == 7,
                               r=[wpm] + xtr(s0, n), w=[psb])
                        t1 = rt1.next()
                        t2 = rt2.next()
                        stt("dve", t1[:, 0:n], psa[:, 0:n], bcol, rtb[:, 0, 0:n], ALU.add, ALU.mult, r=[psa, bfm, rtb], w=[t1])
                        stt("dve", t2[:, 0:n], psb[:, 0:n], bfm[:, swoff + g4:swoff + g4 + 1], rtb[:, 1, 0:n], ALU.add, ALU.mult,
                            r=[psb, bfm, rtb], w=[t2])
                        tt("pool", ot[:, 0:n], t1[:, 0:n], t2[:, 0:n], ALU.add, r=[t1, t2], w=[ot])
                    dma("sp", dst[g4 * 128:(g4 + 1) * 128, d0:d0 + n], ot[:, 0:n], r=[ot])

        allt = [i * 128 for i in range(34)]
        nkt = [3840, 3968] + [i * 128 for i in range(18)] + [4096, 4224]
        for tI in list(range(16)) + [32, 33]:
            ln_tile(tI)
        fm_group(1, QR, gate_scr[0], "silu", NQ)
        fm_group(5, QR, gate_scr[1], "silu", NQ)
        fm_group(7, QR, gate_scr[2], "silu", NQ)
        fm_group(11, QR, gate_scr[3], "silu", NQ)
        rope_group(2, QR, dq_scr, 8, 80)
        fm_group(8, QR, nq_scr, "q8", NQ)
        for g in range(5):
            fm_group(12 + g, QR, mg_scr[g * 512:(g + 1) * 512, :], "sigmoid", NQ)
        for tI in range(16, 32):
            ln_tile(tI)
        for g in range(5, 8):
            fm_group(12 + g, QR, mg_scr[g * 512:(g + 1) * 512, :], "sigmoid", NQ)
        tm_group(0, allt, lambda i: u_scr[i * 128:(i + 1) * 128, :], bu, 1, 512)
        rope_group(3, [(0, 4096, 0), (4096, 256, 4096)], dk_scr, 12, 84)
        tm_group(4, allt, lambda i: dv_scr[i * 128:(i + 1) * 128, :, :], bdv, 4, 128)
        fm_group(6, PLR, pin_scr, "pool", 2320)
        fm_group(9, NKR, nk_scr, "plain", NKN)
        tm_group(10, nkt, lambda i: nv_scr[i * 128:(i + 1) * 128, :, :], bnv, 8, 64)
        P.end()

        if debug == "s2":
            break

        P.begin()
        usb = P.tile("usb", [128, 34, 512], BF16)
        for t0 in range(0, 34, 6):
            t1_ = min(34, t0 + 6)
            dma("sp" if (t0 // 6) % 2 == 0 else "act", usb[:, t0:t1_, :],
                u_scr.rearrange("(t p) c -> p t c", p=128)[:, t0:t1_, :], r=[usb], w=[usb])
        dcb = rot("dcb", [128, 8, 256], BF16, 2)
        dsb = rot("dsb", [128, 8, 256], BF16, 2)
        pacc = [P.ptile("pacc%d" % i, [128, 512], F32) for i in range(4)]
        pch = rot("pch", [128, 256], F32, 2, psum=True)
        pw_ = rot("pw", [128, 256], F32, 2, psum=True)
        ab = rot("ab", [128, 2, 256], BF16, 3)
        fb = rot("fb", [128, 256], BF16, 2)
        gtl = rot("gtl", [128, 4, 256], BF16, 2)
        gout = rot("gout", [128, 4, 256], BF16, 2)
        dcr = dftc.rearrange("(t p) k -> p t k", p=128)
        dsr = dfts.rearrange("(t p) k -> p t k", p=128)
        dc2r = dc256.rearrange("(t p) k -> p t k", p=128)
        ds2r = ds256.rearrange("(t p) k -> p t k", p=128)
        for kq in range(9):
            if debug == "s3a" and kq != 8:
                continue
            if debug == "s3b" and kq >= 2:
                continue
            isctx = kq == 8
            ntl = 2 if isctx else 32
            tbase = 32 if isctx else 0
            qcol = 2048 if isctx else kq * 256
            gt_ = gtl.next()
            dma("act", gt_[:], gate_scr[0].rearrange("(g c) q -> c g q", c=128)[:, :, qcol:qcol + 256], w=[gt_])
            for nt in range(ntl):
                if nt % 8 == 0:
                    dct = dcb.next()
                    dst_ = dsb.next()
                    if isctx:
                        dma("sp", dct[:, 0:2, :], dc2r, w=[dct])
                        dma("pool", dst_[:, 0:2, :], ds2r, w=[dst_])
                    else:
                        dma("sp", dct[:], dcr[:, nt:nt + 8, kq * 256:(kq + 1) * 256], w=[dct])
                        dma("pool", dst_[:], dsr[:, nt:nt + 8, kq * 256:(kq + 1) * 256], w=[dst_])
                for g in range(4):
                    lt = usb[:, tbase + nt, g * 128:(g + 1) * 128]
                    mm(pacc[g][:, 0:256], lt, dct[:, nt % 8, :], nt == 0, nt == ntl - 1, r=[usb, dct], w=[pacc[g]], skip=True)
                    mm(pacc[g][:, 256:512], lt, dst_[:, nt % 8, :], False, nt == ntl - 1, r=[usb, dst_], w=[pacc[g]], skip=True)
            go = gout.next()
            for g in range(4):
                a_ = ab.next()
                cp("dve", a_[:].rearrange("p a b -> p (a b)"), pacc[g][:], r=[pacc[g]], w=[a_])
                pc_ = pch.next()
                mm(pc_[:], cccT[:], a_[:, 0, :], True, False, r=[cccT, a_], w=[pc_])
                mm(pc_[:], nsccT[:], a_[:, 1, :], False, True, r=[nsccT, a_], w=[pc_])
                f_ = fb.next()
                act(f_[:], pc_[:], AF.Identity, r=[pc_], w=[f_])
                pw2 = pw_.next()
                mm(pw2[:], fwT[:, g, :], f_[:], True, True, r=[fwT, f_], w=[pw2])
                tt("dve", go[:, g, :], pw2[:], gt_[:, g, :], ALU.mult, r=[pw2, gt_], w=[go])
            dma("sp", G_scr[0].rearrange("(g c) q -> c g q", c=128)[:, :, qcol:qcol + 256], go[:], r=[go])
        P.end()

        if debug in ("s3", "s3a", "s3b"):
            break
        P.begin()
        pbuf = rot("pbuf", [128, 2320], F32, 2)
        pa = P.tile("pa", [128, 2064], F32)
        pb_ = P.tile("pb", [128, 2064], F32)
        pcx = P.tile("pcx", [128, 272], F32)
        pcxa = P.tile("pcxa", [128, 272], F32)
        pcxb = P.tile("pcxb", [128, 272], F32)
        pinvt = rot("pinvt", [128, NQ], F32, 2)
        pooled = rot("pooled", [128, NQ], BF16, 2)
        pgt = rot("pgt", [128, NQ], BF16, 2)
        pgo = rot("pgo", [128, NQ], BF16, 2)
        pps = rot("pps", [128, 512], F32, 2, psum=True)
        memset("dve", pcx[:], 0.0, w=[pcx])

        def wsum(eng, src, n, g, A, Bt):
            tt(eng, A[:, 1:n + 16], src[:, 0:n + 15], src[:, 1:n + 16], ALU.add, r=[src_t[0], A], w=[A])
            if g == 0:
                return A
            tt(eng, Bt[:, 2:n + 15], A[:, 1:n + 14], A[:, 3:n + 16], ALU.add, r=[A, Bt], w=[Bt])
            if g == 1:
                return Bt
            tt(eng, A[:, 4:n + 13], Bt[:, 2:n + 11], Bt[:, 6:n + 15], ALU.add, r=[Bt, A], w=[A])
            if g == 2:
                return A
            tt(eng, Bt[:, 8:n + 9], A[:, 4:n + 5], A[:, 12:n + 13], ALU.add, r=[A, Bt], w=[Bt])
            return Bt

        src_t = [None]
        for g in range(4):
            pt_ = pbuf.next()
            dma("sp", pt_[:], pin_scr[g * 128:(g + 1) * 128, :], w=[pt_])
            pv_ = pinvt.next()
            dma("act", pv_[:], pinv[g].partition_broadcast(128), w=[pv_])
            gt_ = pgt.next()
            dma("pool", gt_[:], gate_scr[2][g * 128:(g + 1) * 128, :], w=[gt_])
            ts("dve", pt_[:, 0:8], pt_[:, 0:8], pmk[:, 0:1], None, ALU.mult, r=[pt_, pmk], w=[pt_])
            ts("dve", pt_[:, 2056:2064], pt_[:, 2056:2064], pmk[:, 1:2], None, ALU.mult, r=[pt_, pmk], w=[pt_])
            pl = pooled.next()
            src_t[0] = pt_
            S_ = wsum("dve", pt_[:, 0:2064], 2048, g, pa, pb_)
            tt("pool", S_[:, 8:2056], S_[:, 8:2056], pv_[:, 0:2048], ALU.mult, r=[S_, pv_], w=[S_])
            tt("pool", pl[:, 0:2048], S_[:, 8:2056], pt_[:, 8:2056], ALU.subtract, r=[S_, pt_], w=[pl])
            cp("dve", pcx[:, 8:264], pt_[:, 2064:2320], r=[pt_, pcx], w=[pcx])
            src_t[0] = pcx
            S2 = wsum("dve", pcx[:, 0:272], 256, g, pcxa, pcxb)
            tt("pool", S2[:, 8:264], S2[:, 8:264], pv_[:, 2048:2304], ALU.mult, r=[S2, pv_], w=[S2])
            tt("pool", pl[:, 2048:2304], S2[:, 8:264], pcx[:, 8:264], ALU.subtract, r=[S2, pcx, pl], w=[pl])
            go = pgo.next()
            for (s0, n, d0) in chunks([(0, NQ, 0)]):
                ps = pps.next()
                mm(ps[:, 0:n], pwT[:, g, :], pl[:, s0:s0 + n], True, True, r=[pwT, pl], w=[ps])
                stt("dve", go[:, s0:s0 + n], ps[:, 0:n], bfm[:, 112 + g:113 + g], gt_[:, s0:s0 + n], ALU.mult, ALU.mult,
                    r=[ps, bfm, gt_], w=[go])
            dma("sp", G_scr[2][g * 128:(g + 1) * 128, :], go[:], r=[go])
        P.end()

        if debug == "s5":
            break
        P.begin()
        kT = rot("kT", [128, NF], BF16, 2)
        vA = rot("vA", [128, 34, 129], BF16, 2)
        qT = rot("qT", [128, NQ], BF16, 2)
        gdt = rot("gdt", [128, NQ], BF16, 2)
        pS = rot("pS", [128, 512], F32, 3, psum=True)
        pO = [P.ptile("pO%d" % i, [128, 512], F32) for i in range(4)]
        pT = rot("pT", [128, 512], BF16, 1, psum=True)
        PT = rot("PT", [128, 512], BF16, 4)
        oc = rot("oc", [128, 2, 4, 129], F32, 2)
        osm = rot("osm", [128, 16], F32, 4)
        o1 = rot("o1", [128, 128], F32, 2)
        o2 = rot("o2", [128, 128], F32, 2)
        onb = rot("onb", [128, 128], BF16, 8)
        sqj = rot("sqj", [128, 128], F32, 2)
        dgo = rot("dgo", [128, 512], BF16, 2)
        dvr = dv_scr.rearrange("(t p) h e -> p t h e", p=128)
        itc = [0]
        dq = []

        def tick():
            itc[0] += 1
            while dq and dq[0][0] <= itc[0]:
                dq.pop(0)[1]()

        def defer(n, fn):
            dq.append((itc[0] + n, fn))
            dq.sort(key=lambda x_: x_[0])

        def flush():
            while dq:
                dq.pop(0)[1]()

        def d_combine(h, q0, nq_, nqt, oc_, gd_):
            go = dgo.next()
            obs = []
            for qi in range(nqt):
                sm = osm.next()
                P.op("dve", lambda e, sm=sm, oc_=oc_, qi=qi: e.reciprocal(out=sm[:, 0:2], in_=oc_[:, :, qi, 128]),
                     bl([oc_, sm]), bl([sm]))
                ts("dve", sm[:, 1:2], sm[:, 1:2], lamt[:, 5:6], None, ALU.mult, r=[sm, lamt], w=[sm])
                a1 = o1.next()
                a2 = o2.next()
                ts("dve", a1[:], oc_[:, 1, qi, 0:128], sm[:, 1:2], None, ALU.mult, r=[oc_, sm], w=[a1])
                stt("dve", a2[:], oc_[:, 0, qi, 0:128], sm[:, 0:1], a1[:], ALU.mult, ALU.add, r=[oc_, sm, a1], w=[a2])
                sq_ = sqj.next()
                tt("pool", sq_[:], a2[:], a2[:], ALU.mult, r=[a2], w=[sq_])
                P.op("dve", lambda e, sm=sm, sq_=sq_: e.tensor_reduce(out=sm[:, 2:3], in_=sq_[:], axis=AX, op=ALU.add),
                     bl([sq_, sm]), bl([sm]))
                ts("dve", sm[:, 3:4], sm[:, 2:3], 1.0 / 128, 1e-5, ALU.mult, ALU.add, r=[sm], w=[sm])
                act(sm[:, 3:4], sm[:, 3:4], AF.Ln, r=[sm], w=[sm])
                act(sm[:, 3:4], sm[:, 3:4], AF.Exp, scale=-0.5, r=[sm], w=[sm])
                ob = onb.next()
                stt("dve", ob[:], a2[:], sm[:, 3:4], subl[:], ALU.mult, ALU.mult, r=[a2, sm, subl], w=[ob])
                obs.append(ob)

            def d_store():
                for qi in range(nqt):
                    ptt = pT.next()
                    tr(ptt[:, 0:128], obs[qi][:], identB[:], r=[obs[qi], identB], w=[ptt])
                    tt("dve", go[:, qi * 128:(qi + 1) * 128], ptt[:, 0:128], gd_[:, q0 + qi * 128:q0 + (qi + 1) * 128], ALU.mult,
                       r=[ptt, gd_], w=[go])
                dma("sp", G_scr[1][h * 128:(h + 1) * 128, q0:q0 + nq_], go[:, 0:nq_], r=[go])
            defer(6, d_store)

        def d_av(p):
            (h, q0, nq_, nqt, comp, ki, kt_i, last, pt_, va_, oc_, gd_) = p
            for qi in range(nqt):
                bank = pO[comp * 2 + qi // 2]
                c0 = (qi % 2) * 129
                first = (ki == 0 and qi % 2 == 0)
                mm(bank[:, c0:c0 + 129], pt_[:, qi * 128:(qi + 1) * 128], va_[:, kt_i, :], first, last,
                   r=[pt_, va_], w=[bank], skip=True)
            if last:
                for bk in range((nqt + 1) // 2):
                    cp("dve", oc_[:, comp, bk * 2:bk * 2 + 2, :].rearrange("p a b -> p (a b)"),
                       pO[comp * 2 + bk][:, 0:258], r=[pO[comp * 2 + bk]], w=[oc_])
                if comp == 1:
                    defer(4, lambda: d_combine(h, q0, nq_, nqt, oc_, gd_))

        pend = None
        for h in range(4):
            kt_ = kT.next()
            va_ = vA.next()
            qt_ = qT.next()
            gd_ = gdt.next()
            dma("sp", kt_[:], dk_scr[h * 128:(h + 1) * 128, :], w=[kt_])
            dma("pool", va_[:], dvr[:, :, h, :], w=[va_])
            dma("act", qt_[:], dq_scr[h * 128:(h + 1) * 128, :], w=[qt_])
            dma("act", gd_[:], gate_scr[1][h * 128:(h + 1) * 128, :], w=[gd_])
            for qc in range(5):
                isctx = qc == 4
                nq_ = 256 if isctx else 512
                nqt = nq_ // 128
                q0 = 2048 if isctx else qc * 512
                ktl = [32, 33] if isctx else list(range(34))
                oc_ = oc.next()
                for comp in range(2):
                    pr = slice(comp * 64, comp * 64 + 64)
                    for ki, kt_i in enumerate(ktl):
                        ps = pS.next()
                        mm(ps[:, 0:nq_], kt_[pr, kt_i * 128:(kt_i + 1) * 128], qt_[pr, q0:q0 + nq_], True, True,
                           r=[kt_, qt_], w=[ps])
                        pt_ = PT.next()
                        act(pt_[:, 0:nq_], ps[:, 0:nq_], AF.Exp, scale=0.125, r=[ps], w=[pt_])
                        if pend is not None:
                            d_av(pend)
                        pend = (h, q0, nq_, nqt, comp, ki, kt_i, ki == len(ktl) - 1, pt_, va_, oc_, gd_)
                        tick()
        d_av(pend)
        flush()
        flush()
        P.end()

        if debug == "s4":
            break
        P.begin()
        nkT = rot("nkT", [64, NKN], BF16, 2)
        nvA = rot("nvA", [128, 22, 65], BF16, 2)
        nqT = rot("nqT", [64, NQ], BF16, 2)
        ngt = rot("ngt", [64, NQ], BF16, 2)
        tbl = rot("tbl", [128, 5, 768], BF16, 2)
        pSa = rot("pSa", [128, 512], F32, 2, psum=True)
        pSb = rot("pSb", [128, 512], F32, 2, psum=True)
        pOn = rot("pOn", [128, 128], F32, 2, psum=True)
        pTn = rot("pTn", [128, 128], BF16, 1, psum=True)
        PTn = rot("PTn", [128, 1024], BF16, 3)
        nsm = rot("nsm", [128, 2], F32, 4)
        nob = rot("nob", [128, 64], BF16, 4)
        ngo = rot("ngo", [64, NQ], BF16, 2)
        nvr = nv_scr.rearrange("(t p) h e -> p t h e", p=128)
        itc = [0]
        dq = []

        def tick():
            itc[0] += 1
            while dq and dq[0][0] <= itc[0]:
                dq.pop(0)[1]()

        def defer(n, fn):
            dq.append((itc[0] + n, fn))
            dq.sort(key=lambda x_: x_[0])

        def flush():
            while dq:
                dq.pop(0)[1]()

        def n_av(p):
            (h, j, ktl, pt_, nv_, ng_, go, lastj) = p
            qcols = slice(j * 128, (j + 1) * 128)
            po = pOn.next()
            for ai, kt_i in enumerate(ktl):
                mm(po[:, 0:65], pt_[:, ai * 128:(ai + 1) * 128], nv_[:, kt_i, :], ai == 0, ai == len(ktl) - 1,
                   r=[pt_, nv_], w=[po])
            sm = nsm.next()
            P.op("dve", lambda e, sm=sm, po=po: e.reciprocal(out=sm[:, 0:1], in_=po[:, 64:65]), bl([po, sm]), bl([sm]))
            ob = nob.next()
            ts("dve", ob[:], po[:, 0:64], sm[:, 0:1], None, ALU.mult, r=[po, sm], w=[ob])

            def n_store():
                ptt = pTn.next()
                tr(ptt[0:64, 0:128], ob[:], identB[:], r=[ob, identB], w=[ptt])
                tt("dve", go[:, qcols], ptt[0:64, 0:128], ng_[:, qcols], ALU.mult, r=[ptt, ng_], w=[go])
                if lastj:
                    dma("sp", G_scr[3][h * 64:(h + 1) * 64, :], go[:], r=[go])
            defer(2, n_store)

        pend = None
        for h in range(8):
            nk_ = nkT.next()
            nv_ = nvA.next()
            nq_t = nqT.next()
            ng_ = ngt.next()
            tb_ = tbl.next()
            dma("sp", nk_[:], nk_scr[h * 64:(h + 1) * 64, :], w=[nk_])
            dma("pool", nv_[:], nvr[:, :, h, :], w=[nv_])
            dma("act", nq_t[:], nq_scr[h * 64:(h + 1) * 64, :], w=[nq_t])
            dma("act", ng_[:], gate_scr[3][h * 64:(h + 1) * 64, :], w=[ng_])
            dma("pool", tb_[:].rearrange("p a b -> p (a b)"), nab[l, h], w=[tb_])
            go = ngo.next()
            for j in range(18):
                isctx = j >= 16
                qcols = slice(j * 128, (j + 1) * 128)
                pat = {0: 1, 1: 2, 14: 3, 15: 4}.get(j, 0)
                pa_ = pSa.next()
                pb2 = pSb.next()
                if isctx:
                    ktl = [20, 21]
                    nwin = 0
                else:
                    alist = list(range(6)) if j == 0 else (list(range(-1, 5)) if j == 15 else list(range(5)))
                    nwin = len(alist)
                    ktl = [j + a for a in alist] + [20, 21]
                for ai, kt_i in enumerate(ktl):
                    bank = pa_ if ai < 4 else pb2
                    c0 = (ai % 4) * 128
                    hasb = ai < nwin
                    mm(bank[:, c0:c0 + 128], nk_[:, kt_i * 128:(kt_i + 1) * 128], nq_t[:, qcols], True, not hasb,
                       r=[nk_, nq_t], w=[bank])
                    if hasb:
                        mm(bank[:, c0:c0 + 128], identB[:], tb_[:, pat, ai * 128:(ai + 1) * 128], False, True,
                           r=[identB, tb_], w=[bank])
                pt_ = PTn.next()
                if isctx:
                    act(pt_[:, 0:256], pa_[:, 0:256], AF.Exp, r=[pa_], w=[pt_])
                else:
                    nb_ = (len(ktl) - 4) * 128
                    act(pt_[:, 0:512], pa_[:, 0:512], AF.Exp, r=[pa_], w=[pt_])
                    act(pt_[:, 512:512 + nb_], pb2[:, 0:nb_], AF.Exp, r=[pb2, pt_], w=[pt_])
                if pend is not None:
                    n_av(pend)
                pend = (h, j, ktl, pt_, nv_, ng_, go, j == 17)
                tick()
        n_av(pend)
        flush()
        flush()
        P.end()

        if debug == "s6":
            break
        P.begin()
        wbr = P.tile("wbr", [128, 16, D], BF16)
        wo_ = P.tile("wo", [128, 8, D], BF16)
        dma("pool", wbr[:], w_branch[l].rearrange("i (k p) d -> p (i k) d", p=128), w=[wbr])
        dma("pool", wo_[:], w_out[l].rearrange("(k p) d -> p k d", p=128), w=[wo_])
        Gc = rot("Gc", [128, 16, 512], BF16, 2)
        mgt = rot("mgt", [128, 4, 512], BF16, 3)
        mrg = rot("mrg", [128, 8, 512], BF16, 2)
        macc = rot("macc", [128, 512], F32, 2)
        mtmp = rot("mtmp", [128, 512], F32, 3)
        pm = rot("pm", [128, 512], F32, 4, psum=True)
        po2 = rot("po2", [128, 512], F32, 4, psum=True)
        xr = rot("xr", [128, D], F32, 2)
        y1 = rot("y1", [128, D], F32, 2)
        y2 = rot("y2", [128, D], F32, 2)
        y3 = rot("y3", [128, D], F32, 2)
        st7 = rot("st7", [128, 16], F32, 2)
        mgr = mg_scr.rearrange("(i d c) q -> c i d q", i=4, c=128)
        for qc in range(5):
            isctx = qc == 4
            n = 256 if isctx else 512
            q0 = 2048 if isctx else qc * 512
            r_ = 1 if isctx else 0
            gc = Gc.next()
            for i in range(4):
                dma("sp" if i % 2 == 0 else "act", gc[:, i * 4:(i + 1) * 4, 0:n],
                    G_scr[i].rearrange("(k c) q -> c k q", c=128)[:, :, q0:q0 + n], r=[gc], w=[gc])
            mr = mrg.next()
            for dch in range(8):
                mg_ = mgt.next()
                dma("sp", mg_[:, :, 0:n], mgr[:, :, dch, q0:q0 + n], w=[mg_])
                ma = macc.next()
                for i in range(4):
                    ps = pm.next()
                    for kc in range(4):
                        mm(ps[:, 0:n], wbr[:, i * 4 + kc, dch * 128:(dch + 1) * 128], gc[:, i * 4 + kc, 0:n], kc == 0, kc == 3,
                           r=[wbr, gc], w=[ps])
                    if i == 0:
                        tt("dve", ma[:, 0:n], ps[:, 0:n], mg_[:, i, 0:n], ALU.mult, r=[ps, mg_], w=[ma])
                    else:
                        tmp = mtmp.next()
                        tt("dve", tmp[:, 0:n], ps[:, 0:n], mg_[:, i, 0:n], ALU.mult, r=[ps, mg_], w=[tmp])
                        if i < 3:
                            tt("pool", ma[:, 0:n], ma[:, 0:n], tmp[:, 0:n], ALU.add, r=[ma, tmp], w=[ma])
                        else:
                            tt("pool", mr[:, dch, 0:n], ma[:, 0:n], tmp[:, 0:n], ALU.add, r=[ma, tmp], w=[mr])
            for ti in range(n // 128):
                tok0 = q0 + ti * 128
                xt_ = xr.next()
                if l == 0:
                    src = hc[ti * 128:(ti + 1) * 128, :] if isctx else hx[tok0:tok0 + 128, :]
                else:
                    src = hctx[ti * 128:(ti + 1) * 128, :] if isctx else hown[tok0:tok0 + 128, :]
                dma("act", xt_[:], src, w=[xt_])
                ya = y1.next()
                for hh in range(2):
                    ps = po2.next()
                    for kc in range(8):
                        mm(ps[:], mr[:, kc, ti * 128:(ti + 1) * 128], wo_[:, kc, hh * 512:(hh + 1) * 512], kc == 0, kc == 7,
                           r=[mr, wo_], w=[ps])
                    tt("dve", ya[:, hh * 512:(hh + 1) * 512], ps[:], grep[r_][:, hh * 512:(hh + 1) * 512], ALU.mult,
                       r=[ps, grep[r_]], w=[ya])
                yb = y2.next()
                stt("dve", yb[:], xt_[:], ALPHA, ya[:], ALU.mult, ALU.add, r=[xt_, ya], w=[yb])
                s_ = st7.next()
                for hh in range(2):
                    P.op("dve", lambda e, s_=s_, yb=yb, hh=hh: e.bn_stats(out=s_[:, hh * 6:(hh + 1) * 6], in_=yb[:, hh * 512:(hh + 1) * 512]),
                         bl([yb, s_]), bl([s_]))
                P.op("dve", lambda e, s_=s_: e.bn_aggr(out=s_[:, 12:14], in_=s_[:, 0:12]), bl([s_]), bl([s_]))
                ts("dve", s_[:, 14:15], s_[:, 13:14], 1e-6, None, ALU.add, r=[s_], w=[s_])
                act(s_[:, 14:15], s_[:, 14:15], AF.Sqrt, r=[s_], w=[s_])
                P.op("dve", lambda e, s_=s_: e.reciprocal(out=s_[:, 14:15], in_=s_[:, 14:15]), bl([s_]), bl([s_]))
                stt("dve", s_[:, 15:16], s_[:, 12:13], -1.0, s_[:, 14:15], ALU.mult, ALU.mult, r=[s_], w=[s_])
                yc = y3.next()
                act(yc[:], yb[:], AF.Identity, bias=s_[:, 15:16], scale=s_[:, 14:15], r=[s_, yb], w=[yc])
                tt("pool", yc[:], yc[:], lng[:], ALU.mult, r=[yc, lng], w=[yc])
                tt("pool", yc[:], yc[:], lnb[:], ALU.add, r=[yc, lnb], w=[yc])
                if l == L - 1:
                    dma("sp", hout[tok0:tok0 + 128, :], yc[:], r=[yc])
                elif isctx:
                    dma("sp", hctx[ti * 128:(ti + 1) * 128, :], yc[:], r=[yc])
                else:
                    dma("sp", hown[tok0:tok0 + 128, :], yc[:], r=[yc])
                    ym0 = ya
                    ym1 = yb
                    act(ym0[:], yc[:], AF.Identity, scale=pmk[:, 1:2], r=[yc, pmk], w=[ym0])
                    ts("dve", ym1[:], yc[:], pmk[:, 0:1], None, ALU.mult, r=[yc, pmk], w=[ym1])
                    dma("sp", agin[tok0:tok0 + 128, :], ym0[:], r=[ym0, aginb[tok0 // 1024]])
                    dma("act", agin[2048 + tok0:2048 + tok0 + 128, :], ym1[:], r=[ym1, aginb[2 + tok0 // 1024]])
            if l < L - 1 and qc in (1, 3):
                for c4 in ((0, 2) if qc == 1 else (1, 3)):
                    P.dma("cc", None, None, [], [aginb[c4], agoutb[c4]], carry=True,
                          fn=lambda e, c4=c4: e.collective_compute("AllReduce", ALU.add, replica_groups=PAIRS,
                                                                  ins=[agin[c4 * 1024:(c4 + 1) * 1024, :].opt()],
                                                                  outs=[agout[c4 * 1024:(c4 + 1) * 1024, :].opt()]))
        P.end()

    P.finish()
    return nc


_TAB = {}


def _tables(half):
    if half in _TAB:
        return _TAB[half]
    own0 = half * 2048
    oth0 = (1 - half) * 2048
    posF = np.concatenate([np.arange(own0, own0 + 2048), np.arange(oth0, oth0 + 2048)])
    kk = np.arange(own0, own0 + 2048)
    m = (posF[:, None].astype(np.int64) * kk[None, :].astype(np.int64)) % 4096
    ang = m.astype(np.float64) * (2.0 * np.pi / 4096.0)
    dftc = (np.cos(ang) / 64.0).astype(ml_dtypes.bfloat16)
    dfts = (np.sin(ang) / 64.0).astype(ml_dtypes.bfloat16)
    n2 = np.arange(256)
    a2 = ((n2[:, None] * n2[None, :]) % 256).astype(np.float64) * (2.0 * np.pi / 256.0)
    dc256 = (np.cos(a2) / 16.0).astype(ml_dtypes.bfloat16)
    ds256 = (np.sin(a2) / 16.0).astype(ml_dtypes.bfloat16)
    c2 = np.arange(128)
    a3 = ((c2[:, None] * c2[None, :]) % 128).astype(np.float64) * (2.0 * np.pi / 128.0)
    ccc = (np.cos(a3) / np.sqrt(128.0)).astype(ml_dtypes.bfloat16)
    nscc = (-np.sin(a3) / np.sqrt(128.0)).astype(ml_dtypes.bfloat16)
    inv = (np.float32(10000.0) ** (-np.arange(16, dtype=np.float32) / np.float32(16))).astype(np.float32)
    rows = (posF // 64).astype(np.float32)
    cols = (posF % 64).astype(np.float32)
    angs = np.stack([rows[:, None] * inv[None, :], cols[:, None] * inv[None, :]], axis=0).astype(np.float32)
    cosv = np.cos(angs).astype(np.float32)
    sinv = np.sin(angs).astype(np.float32)
    ropec = np.zeros((128, 4096), np.float32)
    ropes = np.zeros((128, 4096), np.float32)
    for comp in range(2):
        for ax in range(2):
            for s in range(2):
                p0 = comp * 64 + ax * 32 + s * 16
                ropec[p0:p0 + 16, :] = cosv[ax].T
                ropes[p0:p0 + 16, :] = (-sinv[ax].T if s == 0 else sinv[ax].T)
    pinv = np.zeros((4, NQ), np.float32)
    for g, w in enumerate((2, 4, 8, 16)):
        t = np.arange(own0, own0 + 2048)
        lo = np.clip(t - w // 2, 0, 4096)
        hi = np.clip(t - w // 2 + w, 0, 4096)
        pinv[g, :2048] = (1.0 / (hi - lo).astype(np.float64)).astype(np.float32)
        t = np.arange(256)
        lo = np.clip(t - w // 2, 0, 256)
        hi = np.clip(t - w // 2 + w, 0, 256)
        pinv[g, 2048:] = (1.0 / (hi - lo).astype(np.float64)).astype(np.float32)
    pmask = np.array([1.0, 0.0] if half == 1 else [0.0, 1.0], np.float32)
    ri = np.zeros((5, 6, 128, 128), np.int64)
    ci = np.zeros((5, 6, 128, 128), np.int64)
    va = np.zeros((5, 6, 128, 128), bool)
    krl = (np.arange(128) // 64)[:, None]
    kc = (np.arange(128) % 64)[:, None]
    qrl = (np.arange(128) // 64)[None, :]
    qc = (np.arange(128) % 64)[None, :]
    for pat, j in enumerate((4, 0, 1, 14, 15)):
        r = 32 * half + 2 * j
        qrow = r + qrl
        rs = np.clip(qrow - 4, 0, 56)
        cs = np.clip(qc - 8, 0, 48)
        alist = list(range(6)) if j == 0 else (list(range(-1, 5)) if j == 15 else list(range(5)))
        for slot, a in enumerate(alist):
            gt = 16 * half + j - 2 + a
            kr = 2 * gt + krl
            ok = (gt >= 0) & (gt <= 31) & (kr >= rs) & (kr <= rs + 7) & (kc >= cs) & (kc <= cs + 15)
            ok = np.broadcast_to(ok, (128, 128))
            ri[pat, slot] = np.clip(np.broadcast_to(kr - qrow + 7, (128, 128)), 0, 14)
            ci[pat, slot] = np.clip(np.broadcast_to(kc - qc + 15, (128, 128)), 0, 30)
            va[pat, slot] = ok
    T = dict(dftc=dftc, dfts=dfts, dc256=dc256, ds256=ds256, ccc=ccc, nscc=nscc, ropec=ropec, ropes=ropes,
             pinv=pinv, pmask=pmask, identf=np.eye(128, dtype=np.float32), ri=ri, ci=ci, va=va)
    _TAB[half] = T
    return T


def _nab_table(na_bias_l, T):
    g = na_bias_l[:, T["ri"], T["ci"]]
    g = np.where(T["va"][None], g, np.float32(-30000.0)).astype(np.float32)
    return np.ascontiguousarray(g.transpose(0, 3, 1, 2, 4)).reshape(8, 128, 3840)


_NC = {}


def _get_nc(n_layers, debug=False):
    key = (n_layers, debug)
    if key not in _NC:
        _NC[key] = build_program(n_layers, debug)
    return _NC[key]


def _core_inputs(core, h_lat, h_ctx, c, c_ctx, W, layers):
    b, half = core // 2, core % 2
    T = _tables(half)
    own = h_lat[b, half * 2048:(half + 1) * 2048]
    oth = h_lat[b, (1 - half) * 2048:(2 - half) * 2048]
    m = {
        "hx": np.ascontiguousarray(np.concatenate([own, oth], axis=0)),
        "hc": np.ascontiguousarray(h_ctx[b]),
        "cvec": np.ascontiguousarray(np.stack([c[b], c_ctx], axis=0).reshape(16, 128)),
    }
    sl = slice(layers[0], layers[-1] + 1)
    for k in ("w_mod", "b_mod", "w_in", "b_in", "fnet_w", "diff_subln", "pool_w", "pool_scale", "w_branch", "w_out", "ln_g", "ln_b"):
        m[k] = np.ascontiguousarray(W[k][sl])
    m["diff_lam"] = np.ascontiguousarray(W["diff_lam"][sl].reshape(len(layers), 256))
    m["nab"] = np.stack([_nab_table(W["na_bias"][l], T) for l in layers], axis=0)
    m["lamc"] = np.array([[0.8 - 0.6 * math.exp(-0.3 * l), 1.0 - (0.8 - 0.6 * math.exp(-0.3 * l))] for l in layers], np.float32)
    for k in ("dftc", "dfts", "dc256", "ds256", "ccc", "nscc", "ropec", "ropes", "pinv", "pmask", "identf"):
        m[k] = T[k]
    return m


def kernel(x, c, ctx, c_ctx, w_mod, b_mod, w_in, b_in, fnet_w, diff_lam, diff_subln, pool_w, pool_scale, na_bias,
           w_branch, w_out, ln_g, ln_b):
    W = dict(w_mod=w_mod, b_mod=b_mod, w_in=w_in, b_in=b_in, fnet_w=fnet_w, diff_lam=diff_lam, diff_subln=diff_subln,
             pool_w=pool_w, pool_scale=pool_scale, na_bias=na_bias, w_branch=w_branch, w_out=w_out, ln_g=ln_g, ln_b=ln_b)
    W = {k: np.asarray(v, np.float32) for k, v in W.items()}
    h_lat = np.asarray(x, np.float32)
    h_ctx = np.asarray(ctx, np.float32)
    c = np.asarray(c, np.float32)
    c_ctx = np.asarray(c_ctx, np.float32)
    nc = _get_nc(4)
    in_maps = [_core_inputs(core, h_lat, h_ctx, c, c_ctx, W, [0, 1, 2, 3]) for core in range(8)]
    res = run_bass_kernel_spmd(nc, in_maps, core_ids=list(range(8)))
    out = np.empty_like(h_lat)
    for core in range(8):
        b, half = core // 2, core % 2
        out[b, half * 2048:(half + 1) * 2048] = np.asarray(res.results[core]["hout"])[0:2048]
    return out
```
